# Optimizing a Trainium2 kernel written in Bass

```python
import math
import jax, jax.numpy as jnp
from jax import lax
import numpy as np

D_MODEL = 1024
BATCH = 4
SEQ = 4096
DEPTH = 2

CHUNK = 64
MEM_LEN = 256
Q_BLOCK = 128
RMS_EPS = 1e-6
NEG_INF = -1e30

FOX_HEAD_DIM = 64
FOX_HEADS = (D_MODEL // 2) // FOX_HEAD_DIM
DIFF_QK_DIM = 64
DIFF_V_DIM = 2 * DIFF_QK_DIM
DIFF_HEADS = (D_MODEL // 2) // DIFF_V_DIM
ATTN_IN_WIDTH = 3 * FOX_HEADS * FOX_HEAD_DIM + FOX_HEADS + DIFF_HEADS * (4 * DIFF_QK_DIM + DIFF_V_DIM)
MLSTM_HEADS = 4
MLSTM_V_DIM = D_MODEL // MLSTM_HEADS
MLSTM_QK_DIM = MLSTM_V_DIM // 2
MLSTM_CONV = 4
MLSTM_IN_WIDTH = 2 * MLSTM_HEADS * MLSTM_QK_DIM + 2 * MLSTM_HEADS * MLSTM_V_DIM + 2 * MLSTM_HEADS
XATTN_HEADS = 4
XATTN_HEAD_DIM = D_MODEL // XATTN_HEADS
D_FF = 256 * math.ceil(8 * D_MODEL / 3 / 256)
FFN_CONV = 3
N_EVEN = (DEPTH + 1) // 2
N_ODD = DEPTH // 2

kernel_name = "fox_diff_mlstm_convffn_hybrid"


def rms_norm(x, gain):
    x32 = x.astype(jnp.float32)
    y = x32 * lax.rsqrt(jnp.mean(x32 * x32, axis=-1, keepdims=True) + RMS_EPS)
    return (y * gain.astype(jnp.float32)).astype(x.dtype)


def causal_depthwise_conv(x, w):
    K = w.shape[0]
    S = x.shape[1]
    xp = jnp.pad(x, ((0, 0), (K - 1, 0), (0, 0)))
    y = xp[:, 0:S, :] * w[0]
    for tap in range(1, K):
        y = y + xp[:, tap:tap + S, :] * w[tap]
    return y


def split_heads(t, n_heads):
    B, S, _ = t.shape
    return t.reshape(B, S, n_heads, -1).transpose(0, 2, 1, 3)


def merge_heads(t):
    B, H, S, d = t.shape
    return t.transpose(0, 2, 1, 3).reshape(B, S, H * d)


def to_blocks(t, size):
    B, H, S = t.shape[:3]
    return jnp.moveaxis(t.reshape(B, H, S // size, size, *t.shape[3:]), 2, 0)


def from_blocks(t):
    nb, B, H, size = t.shape[:4]
    return jnp.moveaxis(t, 0, 2).reshape(B, H, nb * size, *t.shape[4:])


def forgetting_attention(q, k, v, log_f):
    S = q.shape[2]
    scale = q.shape[-1] ** -0.5
    F = jnp.cumsum(log_f, axis=-1)
    k_pos = jnp.arange(S)
    starts = jnp.arange(S // Q_BLOCK) * Q_BLOCK

    def block(args):
        start, q_b, F_b = args
        q_pos = start + jnp.arange(Q_BLOCK)
        logits = jnp.einsum('bhqd,bhkd->bhqk', q_b, k, preferred_element_type=jnp.float32) * scale
        logits = logits + F_b[..., :, None] - F[..., None, :]
        logits = jnp.where(k_pos[None, :] <= q_pos[:, None], logits, NEG_INF)
        p = jax.nn.softmax(logits, axis=-1)
        return jnp.einsum('bhqk,bhkd->bhqd', p.astype(v.dtype), v)

    out = lax.map(block, (starts, to_blocks(q, Q_BLOCK), to_blocks(F, Q_BLOCK)))
    return from_blocks(out)


def differential_attention(q1, q2, k1, k2, v, lam):
    S = q1.shape[2]
    scale = q1.shape[-1] ** -0.5
    k_chunk = jnp.arange(S) // CHUNK
    starts = jnp.arange(S // Q_BLOCK) * Q_BLOCK

    def block(args):
        start, q1_b, q2_b = args
        q_chunk = (start + jnp.arange(Q_BLOCK)) // CHUNK
        mask = k_chunk[None, :] <= q_chunk[:, None]
        l1 = jnp.einsum('bhqd,bhkd->bhqk', q1_b, k1, preferred_element_type=jnp.float32) * scale
        l2 = jnp.einsum('bhqd,bhkd->bhqk', q2_b, k2, preferred_element_type=jnp.float32) * scale
        p1 = jax.nn.softmax(jnp.where(mask, l1, NEG_INF), axis=-1)
        p2 = jax.nn.softmax(jnp.where(mask, l2, NEG_INF), axis=-1)
        p = p1 - lam * p2
        return jnp.einsum('bhqk,bhkd->bhqd', p.astype(v.dtype), v)

    out = lax.map(block, (starts, to_blocks(q1, Q_BLOCK), to_blocks(q2, Q_BLOCK)))
    return from_blocks(out)


def fox_diff_mixer(xn, w_in, fox_bf, lq1, lk1, lq2, lk2, subln, w_out, lambda_init):
    B, S, _ = xn.shape
    proj = xn @ w_in
    fw = FOX_HEADS * FOX_HEAD_DIM
    dqk = DIFF_HEADS * 2 * DIFF_QK_DIM
    cuts = np.cumsum([fw, fw, fw, FOX_HEADS, dqk, dqk]).tolist()
    fq, fk, fv, ff, dq, dk, dv = jnp.split(proj, cuts, axis=-1)
    log_f = jax.nn.log_sigmoid((ff + fox_bf).astype(jnp.float32)).transpose(0, 2, 1)
    fox_out = forgetting_attention(split_heads(fq, FOX_HEADS), split_heads(fk, FOX_HEADS),
                                   split_heads(fv, FOX_HEADS), log_f)
    dq = dq.reshape(B, S, DIFF_HEADS, 2, DIFF_QK_DIM).transpose(0, 2, 3, 1, 4)
    dk = dk.reshape(B, S, DIFF_HEADS, 2, DIFF_QK_DIM).transpose(0, 2, 3, 1, 4)
    lam = (jnp.exp(jnp.sum(lq1.astype(jnp.float32) * lk1.astype(jnp.float32)))
           - jnp.exp(jnp.sum(lq2.astype(jnp.float32) * lk2.astype(jnp.float32))) + lambda_init)
    d_out = differential_attention(dq[:, :, 0], dq[:, :, 1], dk[:, :, 0], dk[:, :, 1],
                                   split_heads(dv, DIFF_HEADS), lam)
    d_out = rms_norm(d_out, subln) * (1.0 - lambda_init)
    mixed = jnp.concatenate([merge_heads(fox_out), merge_heads(d_out)], axis=-1)
    return mixed @ w_out


def mlstm_chunkwise(q, k, v, i_pre, log_f):
    B, H, _, dqk = q.shape
    dv = v.shape[-1]
    causal = jnp.tril(jnp.ones((CHUNK, CHUNK), dtype=bool))

    def step(carry, xs):
        C, n, m = carry
        qc, kc, vc, ic, fc = xs
        b = jnp.cumsum(fc, axis=-1)
        g = b[..., -1]
        D = jnp.where(causal, b[..., :, None] - b[..., None, :] + ic[..., None, :], NEG_INF)
        inter = b + m[..., None]
        m_t = jnp.maximum(inter, jnp.max(D, axis=-1))
        w_inter = jnp.exp(inter - m_t)
        W = jnp.exp(D - m_t[..., None])
        qk = jnp.einsum('bhtd,bhsd->bhts', qc, kc) * W
        num = w_inter[..., None] * jnp.einsum('bhtd,bhde->bhte', qc, C) + jnp.einsum('bhts,bhse->bhte', qk, vc)
        den = w_inter * jnp.einsum('bhtd,bhd->bht', qc, n) + jnp.sum(qk, axis=-1)
        h = num / jnp.maximum(jnp.abs(den), jnp.exp(-m_t))[..., None]
        a = g[..., None] - b + ic
        m_new = jnp.maximum(g + m, jnp.max(a, axis=-1))
        decay = jnp.exp(g + m - m_new)
        wk = jnp.exp(a - m_new[..., None])
        C_new = decay[..., None, None] * C + jnp.einsum('bhs,bhsd,bhse->bhde', wk, kc, vc)
        n_new = decay[..., None] * n + jnp.einsum('bhs,bhsd->bhd', wk, kc)
        return (C_new, n_new, m_new), h

    init = (jnp.zeros((B, H, dqk, dv), jnp.float32), jnp.zeros((B, H, dqk), jnp.float32),
            jnp.zeros((B, H), jnp.float32))
    xs = (to_blocks(q, CHUNK), to_blocks(k, CHUNK), to_blocks(v, CHUNK),
          to_blocks(i_pre, CHUNK), to_blocks(log_f, CHUNK))
    _, h = lax.scan(step, init, xs)
    return from_blocks(h)


def mlstm_mixer(xn, w_in, conv_qk, b_i, b_f, head_norm, w_out):
    proj = xn @ w_in
    qkw = MLSTM_HEADS * MLSTM_QK_DIM
    vw = MLSTM_HEADS * MLSTM_V_DIM
    cuts = np.cumsum([2 * qkw, vw, MLSTM_HEADS, MLSTM_HEADS]).tolist()
    qk_pre, v, ig, fg, og = jnp.split(proj, cuts, axis=-1)
    qk = jax.nn.silu(causal_depthwise_conv(qk_pre, conv_qk))
    q, k = jnp.split(qk, 2, axis=-1)
    q = split_heads(q, MLSTM_HEADS).astype(jnp.float32) * (MLSTM_QK_DIM ** -0.5)
    k = split_heads(k, MLSTM_HEADS).astype(jnp.float32)
    v = split_heads(v, MLSTM_HEADS).astype(jnp.float32)
    i_pre = (ig + b_i).astype(jnp.float32).transpose(0, 2, 1)
    log_f = jax.nn.log_sigmoid((fg + b_f).astype(jnp.float32)).transpose(0, 2, 1)
    h = mlstm_chunkwise(q, k, v, i_pre, log_f)
    h = rms_norm(h, head_norm.reshape(MLSTM_HEADS, 1, MLSTM_V_DIM))
    h = merge_heads(h).astype(xn.dtype) * jax.nn.sigmoid(og)
    return h @ w_out


def memory_cross_attention(xn, memn, wq, wkv, wo):
    q = split_heads(xn @ wq, XATTN_HEADS)
    k, v = jnp.split(memn @ wkv, 2, axis=-1)
    k = split_heads(k, XATTN_HEADS)
    v = split_heads(v, XATTN_HEADS)
    logits = jnp.einsum('bhqd,bhmd->bhqm', q, k, preferred_element_type=jnp.float32) * (XATTN_HEAD_DIM ** -0.5)
    p = jax.nn.softmax(logits, axis=-1)
    out = jnp.einsum('bhqm,bhmd->bhqd', p.astype(v.dtype), v)
    return merge_heads(out) @ wo


def conv_ffn(xn, w_up, conv_w, conv_b, w_down):
    h = causal_depthwise_conv(xn @ w_up, conv_w) + conv_b
    g, u = jnp.split(h, 2, axis=-1)
    return (jax.nn.gelu(g) * u) @ w_down


def setup_inputs(seed: int = 0) -> dict:
    key = jax.random.key(seed)
    ks = iter(jax.random.split(key, 40))

    def nrm(shape, scale):
        return jax.random.normal(next(ks), shape, jnp.float32) * scale

    def gain(shape):
        return 1.0 + nrm(shape, 0.02)

    out_scale = (2.0 * D_MODEL) ** -0.5
    return {
        'x': nrm((BATCH, SEQ, D_MODEL), 1.0),
        'mem': nrm((BATCH, MEM_LEN, D_MODEL), 1.0),
        'mix_norm': gain((DEPTH, D_MODEL)),
        'xattn_norm': gain((DEPTH, D_MODEL)),
        'mem_norm': gain((DEPTH, D_MODEL)),
        'ffn_norm': gain((DEPTH, D_MODEL)),
        'attn_w_in': nrm((N_EVEN, D_MODEL, ATTN_IN_WIDTH), D_MODEL ** -0.5),
        'attn_fox_bf': jnp.linspace(1.0, 4.0, FOX_HEADS, dtype=jnp.float32) + nrm((N_EVEN, FOX_HEADS), 0.1),
        'diff_lq1': nrm((N_EVEN, DIFF_QK_DIM), 0.1),
        'diff_lk1': nrm((N_EVEN, DIFF_QK_DIM), 0.1),
        'diff_lq2': nrm((N_EVEN, DIFF_QK_DIM), 0.1),
        'diff_lk2': nrm((N_EVEN, DIFF_QK_DIM), 0.1),
        'diff_subln': gain((N_EVEN, DIFF_V_DIM)),
        'attn_w_out': nrm((N_EVEN, D_MODEL, D_MODEL), out_scale),
        'mlstm_w_in': nrm((N_ODD, D_MODEL, MLSTM_IN_WIDTH), D_MODEL ** -0.5),
        'mlstm_conv_qk': nrm((N_ODD, MLSTM_CONV, 2 * MLSTM_HEADS * MLSTM_QK_DIM), MLSTM_CONV ** -0.5),
        'mlstm_b_i': nrm((N_ODD, MLSTM_HEADS), 0.1),
        'mlstm_b_f': jnp.linspace(3.0, 6.0, MLSTM_HEADS, dtype=jnp.float32) + nrm((N_ODD, MLSTM_HEADS), 0.1),
        'mlstm_head_norm': gain((N_ODD, MLSTM_HEADS * MLSTM_V_DIM)),
        'mlstm_w_out': nrm((N_ODD, D_MODEL, D_MODEL), out_scale),
        'xattn_wq': nrm((DEPTH, D_MODEL, D_MODEL), D_MODEL ** -0.5),
        'xattn_wkv': nrm((DEPTH, D_MODEL, 2 * D_MODEL), D_MODEL ** -0.5),
        'xattn_wo': nrm((DEPTH, D_MODEL, D_MODEL), out_scale),
        'ffn_w_up': nrm((DEPTH, D_MODEL, 2 * D_FF), D_MODEL ** -0.5),
        'ffn_conv_w': nrm((DEPTH, FFN_CONV, 2 * D_FF), FFN_CONV ** -0.5),
        'ffn_conv_b': nrm((DEPTH, 2 * D_FF), 0.02),
        'ffn_w_down': nrm((DEPTH, D_FF, D_MODEL), (2.0 * D_FF) ** -0.5),
        'final_norm': gain((D_MODEL,)),
    }


def reference(x, mem, mix_norm, xattn_norm, mem_norm, ffn_norm,
              attn_w_in, attn_fox_bf, diff_lq1, diff_lk1, diff_lq2, diff_lk2, diff_subln, attn_w_out,
              mlstm_w_in, mlstm_conv_qk, mlstm_b_i, mlstm_b_f, mlstm_head_norm, mlstm_w_out,
              xattn_wq, xattn_wkv, xattn_wo,
              ffn_w_up, ffn_conv_w, ffn_conv_b, ffn_w_down, final_norm):
    for layer in range(DEPTH):
        j = layer // 2
        xn = rms_norm(x, mix_norm[layer])
        if layer % 2 == 0:
            lambda_init = 0.8 - 0.6 * math.exp(-0.3 * layer)
            mixed = fox_diff_mixer(xn, attn_w_in[j], attn_fox_bf[j], diff_lq1[j], diff_lk1[j],
                                   diff_lq2[j], diff_lk2[j], diff_subln[j], attn_w_out[j], lambda_init)
        else:
            mixed = mlstm_mixer(xn, mlstm_w_in[j], mlstm_conv_qk[j], mlstm_b_i[j], mlstm_b_f[j],
                                mlstm_head_norm[j], mlstm_w_out[j])
        x = x + mixed
        x = x + memory_cross_attention(rms_norm(x, xattn_norm[layer]), rms_norm(mem, mem_norm[layer]),
                                       xattn_wq[layer], xattn_wkv[layer], xattn_wo[layer])
        x = x + conv_ffn(rms_norm(x, ffn_norm[layer]), ffn_w_up[layer], ffn_conv_w[layer],
                         ffn_conv_b[layer], ffn_w_down[layer])
    return rms_norm(x, final_norm)
```

```python
import math
import numpy as np
import ml_dtypes
import concourse.bass as bass
import concourse.mybir as mybir
from concourse.bass_utils import run_bass_kernel_spmd

F32 = mybir.dt.float32
BF16 = mybir.dt.bfloat16
AF = mybir.ActivationFunctionType
ALU = mybir.AluOpType

SAME_ENGINE_SYNC = True
EPOCH = 30000
NDMA_SEMS = 12

T = 4096
D = 1024
NT = 8
TS = 512
NB = 32
DFF = 2816
NH = 22
EPS = 1e-6
NEG = -30000.0


class Buf:
    def __init__(self, name):
        self.name = name
        self.writers = []
        self.readers = []


def _compress(toks):
    mx = {}
    for k, v in toks:
        if mx.get(k, 0) < v:
            mx[k] = v
    return list(mx.items())


class Sched:
    ENGS = ("tensor", "scalar", "vector", "gpsimd", "sync")

    def __init__(self, nc):
        self.nc = nc
        self.ops = {e: [] for e in self.ENGS}
        self.sems = {}
        self.eng_epoch = {e: 0 for e in self.ENGS}
        self.eng_cnt = {e: 0 for e in self.ENGS}
        self.known = {e: {} for e in self.ENGS}
        self.dma_rr = {e: 0 for e in self.ENGS}
        self.dma_cnt = {}
        self.all_tokens = {}
        self.n_ops = 0

    def _sem(self, key):
        if key not in self.sems:
            self.sems[key] = self.nc.alloc_semaphore(name="s_" + "_".join(str(k) for k in key))
        return self.sems[key]

    def _collect(self, eng, reads, writes, sreads=(), chain=False, accum=False):
        need = {}
        force = {}
        for b_ in sreads:
            for k, v in b_.writers:
                if force.get(k, 0) < v:
                    force[k] = v

        def add(tok):
            k, v = tok
            if need.get(k, 0) < v:
                need[k] = v
        for b in reads:
            for t in b.writers:
                add(t)
        for b in sreads:
            for t in b.writers:
                add(t)
        for b in writes:
            for t in b.writers:
                if chain and t[0][0] == "e" and t[0][1] == eng:
                    continue
                if accum and t[0][0] == "d":
                    continue
                add(t)
            for t in b.readers:
                add(t)
        waits = []
        for k, v in need.items():
            if k[0] == "e" and k[1] == eng and not SAME_ENGINE_SYNC and force.get(k, 0) < v:
                continue
            if self.known[eng].get(k, 0) >= v:
                continue
            self.known[eng][k] = v
            waits.append((k, v))
        return waits

    def _update(self, tok, reads, writes, accum=False):
        for b in reads:
            b.readers.append(tok)
            if len(b.readers) > 48:
                b.readers = _compress(b.readers)
        for b in writes:
            if accum:
                b.writers.append(tok)
                if len(b.writers) > 48:
                    b.writers = _compress(b.writers)
            else:
                b.writers = [tok]
                b.readers = []
        self.all_tokens[tok[0]] = tok[1]

    def op(self, eng, fn, reads=(), writes=(), sreads=(), chain=False):
        if eng == "tensor":
            chain = True
        waits = self._collect(eng, reads, writes, sreads, chain)
        reads = list(reads) + list(sreads)
        if self.eng_cnt[eng] >= EPOCH:
            self.eng_epoch[eng] += 1
            self.eng_cnt[eng] = 0
        self.eng_cnt[eng] += 1
        key = ("e", eng, self.eng_epoch[eng])
        tok = (key, self.eng_cnt[eng])
        self._sem(key)
        self.ops[eng].append((waits, fn, (key, 1)))
        self._update(tok, reads, writes)
        self.n_ops += 1
        return tok

    def dma(self, eng, fn, reads=(), writes=(), accum=False):
        waits = self._collect(eng, reads, writes, accum=accum)
        i = self.dma_rr[eng]
        self.dma_rr[eng] = (i + 1) % NDMA_SEMS
        key = ("d", eng, i)
        self._sem(key)
        prev = self.dma_cnt.get(key, 0)
        if prev > 0 and self.known[eng].get(key, 0) < prev:
            self.known[eng][key] = prev
            waits.append((key, prev))
        val = prev + 16
        self.dma_cnt[key] = val
        tok = (key, val)
        self.ops[eng].append((waits, fn, (key, 16)))
        self._update(tok, reads, writes, accum=accum)
        self.n_ops += 1
        return tok

    def barrier(self):
        for eng in self.ENGS:
            waits = []
            for k, v in self.all_tokens.items():
                if k[0] == "e" and k[1] == eng:
                    continue
                if self.known[eng].get(k, 0) >= v:
                    continue
                self.known[eng][k] = v
                waits.append((k, v))
            if waits:
                self.ops[eng].append((waits, None, None))

    def emit(self):
        nc = self.nc
        self.barrier()
        sems = self.sems
        with nc.Block() as block:
            def body(eng_name):
                def f(e):
                    for waits, fn, inc in self.ops[eng_name]:
                        for k, v in waits:
                            e.wait_ge(sems[k], v)
                        if fn is not None:
                            ins = fn(e)
                            ins.then_inc(sems[inc[0]], inc[1])
                return f
            block.sync(body("sync"))
            block.tensor(body("tensor"))
            block.scalar(body("scalar"))
            block.vector(body("vector"))
            block.gpsimd(body("gpsimd"))


class TT:
    def __init__(self, ap, name):
        self.ap = ap
        self.b = Buf(name)

    def __getitem__(self, k):
        return self.ap[k]


class Arena:
    def __init__(self, raw_ap, nwords):
        self.raw = raw_ap
        self.n = nwords
        self.off = 0
        self.cnt = 0

    def reset(self):
        self.off = 0

    def alloc(self, shape, dtype, name=None):
        free = 1
        for s in shape[1:]:
            free *= s
        words = free if dtype == F32 else (free + 1) // 2
        words = (words + 7) // 8 * 8
        assert self.off + words <= self.n, f"arena overflow {name} {self.off}+{words}>{self.n}"
        v = self.raw[0:shape[0], self.off:self.off + words]
        self.off += words
        if dtype != F32:
            v = v.bitcast(BF16)
        v = v[:, 0:free]
        if len(shape) == 3:
            v = v.rearrange("p (a b) -> p a b", a=shape[1])
        elif len(shape) == 4:
            v = v.rearrange("p (a b c) -> p a b c", a=shape[1], b=shape[2])
        self.cnt += 1
        return TT(v, name or f"ar{self.cnt}")


class Builder:
    def __init__(self, stop_after=None, debug=False):
        self.stop_after = stop_after
        self.debug = debug
        nc = bass.Bass("TRN2", target_bir_lowering=False)
        self.nc = nc
        self.S = Sched(nc)
        self.rr = 0
        self._declare_io()
        self._alloc()

    def _din(self, name, shape, dtype=F32):
        return self.nc.dram_tensor(name, list(shape), dtype, kind="ExternalInput").ap()

    def _dscr(self, name, shape, dtype):
        if self.debug:
            return TT(self.nc.dram_tensor(name, list(shape), dtype, kind="ExternalOutput").ap(), name)
        return TT(self.nc.dram_tensor(name, list(shape), dtype).ap(), name)

    def _declare_io(self):
        d = self._din
        self.x_in = d("x", [T, D])
        self.mem_in = d("mem", [256, D])
        self.gains = d("gains", [128, 9, 8])
        self.memgain = d("memgain", [2, D])
        self.attn_w_in = d("attn_w_in", [D, 3080])
        self.attn_w_out = d("attn_w_out", [D, D])
        self.mlstm_w_in = d("mlstm_w_in", [D, 3080])
        self.mlstm_w_out = d("mlstm_w_out", [D, D])
        self.xattn_wq = d("xattn_wq", [2, D, D])
        self.xattn_wkv = d("xattn_wkv", [2, D, 2 * D])
        self.xattn_wo = d("xattn_wo", [2, D, D])
        self.ffn_w_up = d("ffn_w_up", [2, D, 2 * DFF])
        self.ffn_w_down = d("ffn_w_down", [2, DFF, D])
        self.ffn_conv_w = d("ffn_conv_w", [2, 128, 3, 44])
        self.ffn_conv_b = d("ffn_conv_b", [2, 128, 44])
        self.fox_bf = d("fox_bf", [8, 1])
        self.lam_in = d("lam_in", [4, 64])
        self.subln = d("subln", [128, 1])
        self.mconv = d("mconv", [128, 4, 8])
        self.mgate_b = d("mgate_b", [4, 2])
        self.mhn = d("mhn", [128, 8])
        self.c_ident = d("c_ident", [128, 128])
        self.c_trineg = d("c_trineg", [128, 128])
        self.c_dneg = d("c_dneg", [128, 128])
        self.c_tri01 = d("c_tri01", [128, 128])
        self.c_sel = d("c_sel", [4, 4, 128])
        self.out = self.nc.dram_tensor("out", [T, D], F32, kind="ExternalOutput").ap()
        self.b_out = Buf("out")
        s = self._dscr
        self.XT = s("XT", [D, T], F32)
        self.QA = s("QA", [8, 70, T], BF16)
        self.KA = s("KA", [8, 70, T], BF16)
        self.VA = s("VA", [T, 768], BF16)
        self.DQ = s("DQ", [4, 128, T], BF16)
        self.DK = s("DK", [4, 128, T], BF16)
        self.DV = s("DV", [T, 512], BF16)
        self.MIX = s("MIX", [D, T], BF16)
        self.QK = s("QK", [8, 128, T], BF16)
        self.VM = s("VM", [T, 1028], BF16)
        self.OG = s("OG", [D, T], BF16)
        self.DBG = s("DBG", [8, 128, TS], F32)

    def _alloc(self):
        nc = self.nc
        S = self.S

        def sb(name, shape, dtype):
            return TT(nc.alloc_sbuf_tensor(name, list(shape), dtype)[:], name)
        self.ident = sb("ident", [128, 128], F32)
        self.identb = sb("identb", [128, 128], BF16)
        self.onesb = sb("onesb", [128, 128], BF16)
        self.trineg = sb("trineg", [128, 128], F32)
        self.dneg = sb("dneg", [128, 128], F32)
        self.tri01 = sb("tri01", [128, 128], F32)
        self.sel = sb("sel", [4, 4, 128], F32)
        self.gain = sb("gain", [128, 9, 8], F32)
        self.epsb = sb("epsb", [128, 1], F32)
        self.small = sb("small", [128, 64], F32)
        self.ps = [TT(nc.alloc_psum_tensor(f"ps{i}", [128, 512], F32)[:], f"ps{i}") for i in range(8)]
        nwords = (nc.sbuf_bytes_remaining - 2048) // 4
        raw = nc.alloc_sbuf_tensor("arena", [128, nwords], F32)
        self.ar = Arena(raw[:], nwords)
        ld = lambda dst, src: S.dma("sync", lambda e: e.dma_start(out=dst.ap, in_=src), writes=[dst.b], accum=True)
        ld(self.ident, self.c_ident)
        ld(self.trineg, self.c_trineg)
        ld(self.dneg, self.c_dneg)
        ld(self.tri01, self.c_tri01)
        ld(self.sel, self.c_sel)
        S.dma("sync", lambda e: e.dma_start(out=self.gain.ap, in_=self.gains),
              writes=[self.gain.b])
        S.op("vector", lambda e: e.tensor_copy(out=self.identb.ap, in_=self.ident.ap), reads=[self.ident.b], writes=[self.identb.b])
        S.op("vector", lambda e: e.memset(self.onesb.ap, 1.0), writes=[self.onesb.b])
        S.op("vector", lambda e: e.memset(self.epsb.ap, EPS), writes=[self.epsb.b])

    def psum(self):
        p = self.ps[self.rr % 8]
        self.rr += 1
        return p

    def load_w(self, dst, src_ap, kc_list=None):
        S = self.S
        KC = dst.ap.shape[1]
        v = src_ap.rearrange("(kc p) n -> p kc n", p=128)
        for kc in range(KC):
            S.dma("gpsimd", lambda e, kc=kc: e.dma_start(out=dst.ap[:, kc, :], in_=v[:, kc, :]), writes=[dst.b], accum=True)

    def evac(self, i, out_ap, in_ap, reads, writes, scale=None):
        S = self.S
        if i % 2 == 0:
            if scale is None:
                S.op("scalar", lambda e: e.copy(out=out_ap, in_=in_ap), reads=reads, writes=writes)
            else:
                S.op("scalar", lambda e: e.mul(out=out_ap, in_=in_ap, mul=scale), reads=reads, writes=writes)
        else:
            if scale is None:
                S.op("vector", lambda e: e.tensor_copy(out=out_ap, in_=in_ap), reads=reads, writes=writes)
            else:
                S.op("vector", lambda e: e.tensor_scalar(out=out_ap, in0=in_ap, scalar1=scale, scalar2=None, op0=ALU.mult),
                     reads=reads, writes=writes)

    def rmsnorm(self, xt, xn, sq, rs, gidx, nfeat=D, p=None):
        S = self.S
        S.op("scalar", lambda e: e.activation(out=sq.ap, in_=xt.ap, func=AF.Square), reads=[xt.b], writes=[sq.b])
        if p is None:
            p = self.psum()
        N = xt.ap.shape[2]
        for c in range(8):
            S.op("tensor", lambda e, c=c: e.matmul(out=p.ap[:, 0:N], lhsT=self.onesb.ap, rhs=sq.ap[:, c, :], start=(c == 0), stop=(c == 7)),
                 reads=[self.onesb.b, sq.b], writes=[p.b])
        S.op("scalar", lambda e: e.activation(out=rs.ap, in_=p.ap[:, 0:N], func=AF.Sqrt, bias=self.epsb.ap[:, 0:1], scale=1.0 / nfeat),
             reads=[p.b, self.epsb.b], writes=[rs.b])
        S.op("vector", lambda e: e.reciprocal(out=rs.ap, in_=rs.ap), reads=[rs.b], writes=[rs.b])
        for c in range(8):
            eng = "vector"
            S.op(eng, lambda e, c=c: e.scalar_tensor_tensor(out=xn.ap[:, c, :], in0=xt.ap[:, c, :], scalar=self.gain.ap[:, gidx, c:c + 1],
                                                              in1=rs.ap, op0=ALU.mult, op1=ALU.mult),
                 reads=[xt.b, rs.b, self.gain.b], writes=[xn.b])

    def proj_fm(self, p, w, col0, M, xn, N=TS, prow=0):
        S = self.S
        KC = w.ap.shape[1]
        for kc in range(KC):
            S.op("tensor", lambda e, kc=kc: e.matmul(out=p.ap[prow:prow + M, 0:N], lhsT=w.ap[:, kc, col0:col0 + M], rhs=xn.ap[:, kc, 0:N],
                                                     start=(kc == 0), stop=(kc == KC - 1)),
                 reads=[w.b, xn.b], writes=[p.b])

    def proj_tm(self, p, xn, tok0, w, col0, ncol):
        S = self.S
        KC = w.ap.shape[1]
        for kc in range(KC):
            S.op("tensor", lambda e, kc=kc: e.matmul(out=p.ap[:, 0:ncol], lhsT=xn.ap[:, kc, tok0:tok0 + 128], rhs=w.ap[:, kc, col0:col0 + ncol],
                                                     start=(kc == 0), stop=(kc == KC - 1)),
                 reads=[w.b, xn.b], writes=[p.b])

    def xt_tile_ap(self, i):
        return self.XT.ap.rearrange("(c p) t -> p c t", p=128)[:, :, i * TS:(i + 1) * TS]

    def phase_l0_inproj(self):
        S = self.S
        ar = self.ar
        ar.reset()
        lfn = ar.alloc([8, T], F32, "lfn")
        self.lfn = lfn
        nbf = ar.alloc([8, 1], F32, "nbf")
        tmp8 = ar.alloc([8, TS], F32, "tmp8")
        mark = ar.off
        w = ar.alloc([128, 8, 3080], BF16, "w_in0")
        self.load_w(w, self.attn_w_in)
        xtm = [ar.alloc([128, 4, D], F32, f"xtm{k}") for k in range(2)]
        xt = [ar.alloc([128, 8, TS], F32, f"xt{k}") for k in range(2)]
        xn = [ar.alloc([128, 8, TS], BF16, f"xn{k}") for k in range(2)]
        sq = ar.alloc([128, 8, TS], BF16, "sq")
        rs = ar.alloc([128, TS], F32, "rs")
        stg = [ar.alloc([128, TS], BF16, f"stg{k}") for k in range(6)]
        vst = [ar.alloc([128, 4, 768], BF16, f"vst{k}") for k in range(2)]
        dvst = [ar.alloc([128, 4, 512], BF16, f"dvst{k}") for k in range(2)]
        for k in range(2):
            S.op("gpsimd", lambda e, k=k: e.memset(vst[k].ap, 1.0), writes=[vst[k].b])
        S.dma("sync", lambda e: e.dma_start(out=nbf.ap, in_=self.fox_bf), writes=[nbf.b])
        S.op("vector", lambda e: e.tensor_scalar(out=nbf.ap, in0=nbf.ap, scalar1=-1.0, scalar2=None, op0=ALU.mult), reads=[nbf.b], writes=[nbf.b])
        si = 0
        ev = 0
        evb = [0]

        def prep(i):
            xm = xtm[i % 2]
            x_ = xt[i % 2]
            xn_ = xn[i % 2]
            S.dma("sync", lambda e: e.dma_start(out=xm.ap, in_=self.x_in[i * TS:(i + 1) * TS, :].rearrange("(s p) d -> p s d", p=128)), writes=[xm.b])
            for c in range(8):
                p = self.psum()
                for s_ in range(4):
                    S.op("tensor", lambda e, s_=s_, c=c, p=p: e.transpose(out=p.ap[:, s_ * 128:(s_ + 1) * 128], in_=xm.ap[:, s_, c * 128:(c + 1) * 128], identity=self.ident.ap),
                         reads=[xm.b, self.ident.b], writes=[p.b])
                self.evac(evb[0], x_.ap[:, c, :], p.ap, [p.b], [x_.b]); evb[0] += 1
            S.dma("sync", lambda e: e.dma_start(out=self.xt_tile_ap(i), in_=x_.ap), reads=[x_.b], writes=[self.XT.b], accum=True)
            self.rmsnorm(x_, xn_, sq, rs, 0)

        prep(0)
        for i in range(NT):
            x_ = xt[i % 2]
            xn_ = xn[i % 2]
            if i + 1 < NT:
                prep(i + 1)
            for grp, (col0, dst, scale) in enumerate([(0, self.QA, 0.125), (512, self.KA, None), (1544, self.DQ, 0.125), (2056, self.DK, None)]):
                for j in range(4):
                    p = self.psum()
                    self.proj_fm(p, w, col0 + j * 128, 128, xn_)
                    st = stg[si % 6]; si += 1
                    self.evac(ev, st.ap, p.ap, [p.b], [st.b], scale=scale); ev += 1
                    if grp < 2:
                        for hh in range(2):
                            S.dma("sync", lambda e, st=st, dst=dst, j=j, hh=hh, i=i: e.dma_start(
                                out=dst.ap[2 * j + hh, 0:64, i * TS:(i + 1) * TS], in_=st.ap[hh * 64:(hh + 1) * 64, :]),
                                reads=[st.b], writes=[dst.b], accum=True)
                    else:
                        S.dma("sync", lambda e, st=st, dst=dst, j=j, i=i: e.dma_start(out=dst.ap[j, :, i * TS:(i + 1) * TS], in_=st.ap),
                              reads=[st.b], writes=[dst.b], accum=True)
            vs = vst[i % 2]
            ds = dvst[i % 2]
            for s_ in range(4):
                p = self.psum()
                self.proj_tm(p, xn_, s_ * 128, w, 1024, 512)
                pv = p.ap.rearrange("p (g e c) -> p g e c", g=4, e=2)
                ov = vs.ap[:, s_, :].rearrange("p (g c) -> p g c", c=192)
                S.op("scalar", lambda e, ov=ov, pv=pv: e.copy(out=ov[:, :, 0:64], in_=pv[:, :, 0, :]), reads=[p.b], writes=[vs.b])
                S.op("vector", lambda e, ov=ov, pv=pv: e.tensor_copy(out=ov[:, :, 128:192], in_=pv[:, :, 1, :]), reads=[p.b], writes=[vs.b])
                p2 = self.psum()
                self.proj_tm(p2, xn_, s_ * 128, w, 2568, 512)
                self.evac(ev, ds.ap[:, s_, :], p2.ap, [p2.b], [ds.b]); ev += 1
            S.dma("sync", lambda e, vs=vs, i=i: e.dma_start(out=self.VA.ap[i * TS:(i + 1) * TS, :].rearrange("(s p) c -> p s c", p=128), in_=vs.ap),
                  reads=[vs.b], writes=[self.VA.b], accum=True)
            S.dma("sync", lambda e, ds=ds, i=i: e.dma_start(out=self.DV.ap[i * TS:(i + 1) * TS, :].rearrange("(s p) c -> p s c", p=128), in_=ds.ap),
                  reads=[ds.b], writes=[self.DV.b], accum=True)
            p = self.psum()
            self.proj_fm(p, w, 1536, 8, xn_)
            S.op("scalar", lambda e, p=p: e.activation(out=tmp8.ap, in_=p.ap[0:8, :], func=AF.Exp, bias=nbf.ap[:, 0:1], scale=-1.0),
                 reads=[p.b, nbf.b], writes=[tmp8.b])
            S.op("scalar", lambda e, i=i: e.activation(out=lfn.ap[:, i * TS:(i + 1) * TS], in_=tmp8.ap, func=AF.Ln, bias=1.0, scale=1.0),
                 reads=[tmp8.b], writes=[lfn.b])
        S.barrier()
        ar.off = mark
        cs = ar.alloc([8, T], F32, "cs")
        ones8 = ar.alloc([8, T], F32, "ones8")
        r32 = ar.alloc([8, T], F32, "r32")
        t32 = ar.alloc([8, T], F32, "t32")
        parts = [ar.alloc([8, T], BF16, f"part{k}") for k in range(6)]
        onesb8 = ar.alloc([8, T], BF16, "onesb8")
        S.op("gpsimd", lambda e: e.memset(ones8.ap, 1.0), writes=[ones8.b])
        S.op("gpsimd", lambda e: e.memset(onesb8.ap, 1.0), writes=[onesb8.b])
        for i in range(NT):
            init = 0.0 if i == 0 else cs.ap[:, i * TS - 1:i * TS]
            S.op("vector", lambda e, i=i, init=init: e.tensor_tensor_scan(out=cs.ap[:, i * TS:(i + 1) * TS], data0=ones8.ap[:, i * TS:(i + 1) * TS],
                                                                    data1=lfn.ap[:, i * TS:(i + 1) * TS], initial=init, op0=ALU.mult, op1=ALU.add),
                 reads=[ones8.b, lfn.b], sreads=[cs.b], writes=[cs.b])
        V = "vector"
        S.op(V, lambda e: e.tensor_copy(out=parts[0].ap, in_=cs.ap), reads=[cs.b], writes=[parts[0].b])
        S.op(V, lambda e: e.tensor_copy(out=t32.ap, in_=parts[0].ap), reads=[parts[0].b], writes=[t32.b])
        S.op(V, lambda e: e.tensor_tensor(out=r32.ap, in0=cs.ap, in1=t32.ap, op=ALU.subtract), reads=[cs.b, t32.b], writes=[r32.b])
        S.op(V, lambda e: e.tensor_copy(out=parts[1].ap, in_=r32.ap), reads=[r32.b], writes=[parts[1].b])
        S.op(V, lambda e: e.tensor_copy(out=t32.ap, in_=parts[1].ap), reads=[parts[1].b], writes=[t32.b])
        S.op(V, lambda e: e.tensor_tensor(out=r32.ap, in0=r32.ap, in1=t32.ap, op=ALU.subtract), reads=[r32.b, t32.b], writes=[r32.b])
        S.op(V, lambda e: e.tensor_copy(out=parts[2].ap, in_=r32.ap), reads=[r32.b], writes=[parts[2].b])
        for k in range(3):
            S.op(V, lambda e, k=k: e.tensor_scalar(out=parts[3 + k].ap, in0=parts[k].ap, scalar1=-1.0, scalar2=None, op0=ALU.mult),
                 reads=[parts[k].b], writes=[parts[3 + k].b])
        for r in range(3):
            S.dma("sync", lambda e, r=r: e.dma_start(out=self.QA.ap[:, 64 + r, :], in_=parts[3 + r].ap), reads=[parts[3 + r].b], writes=[self.QA.b], accum=True)
            S.dma("sync", lambda e, r=r: e.dma_start(out=self.QA.ap[:, 67 + r, :], in_=onesb8.ap), reads=[onesb8.b], writes=[self.QA.b], accum=True)
            S.dma("sync", lambda e, r=r: e.dma_start(out=self.KA.ap[:, 64 + r, :], in_=onesb8.ap), reads=[onesb8.b], writes=[self.KA.b], accum=True)
            S.dma("sync", lambda e, r=r: e.dma_start(out=self.KA.ap[:, 67 + r, :], in_=parts[r].ap), reads=[parts[r].b], writes=[self.KA.b], accum=True)
        S.barrier()

    def phase_fox(self):
        S = self.S
        ar = self.ar
        ar.reset()
        va = ar.alloc([128, NB, 768], BF16, "va")
        S.dma("sync", lambda e: e.dma_start(out=va.ap, in_=self.VA.ap.rearrange("(b p) c -> p b c", p=128)), reads=[self.VA.b], writes=[va.b])
        qa = [ar.alloc([70, T], BF16, f"qa{k}") for k in range(2)]
        ka = [ar.alloc([70, T], BF16, f"ka{k}") for k in range(2)]
        pt = [ar.alloc([128, TS], BF16, f"pt{k}") for k in range(6)]
        rd = [ar.alloc([128, TS], F32, f"rd{k}") for k in range(2)]
        rd2 = [ar.alloc([128, TS], F32, f"rd2{k}") for k in range(2)]
        ost = [ar.alloc([128, TS], BF16, f"ost{k}") for k in range(2)]
        acc = [self.ps[6], self.ps[7]]
        sps = self.ps[0:6]
        A, B = [], []
        u = 0
        tcount = 0
        for h in range(8):
            q_ = qa[h % 2]
            k_ = ka[h % 2]
            pair = h // 2
            odd = h % 2
            vc0 = pair * 192 + (64 if odd else 0)
            nrow = slice(64, 128) if odd else slice(0, 64)
            drow = slice(0, 64) if odd else slice(64, 128)
            for i in range(NT):
                a = acc[tcount % 2]
                rd_ = rd[tcount % 2]
                rd2_ = rd2[tcount % 2]
                os_ = ost[tcount % 2]
                tcount += 1
                nj = 4 * i + 4
                for j in range(nj):
                    r = j - 4 * i
                    c0 = 128 * r if r > 0 else 0
                    sp = sps[u % 6]
                    p_ = pt[u % len(pt)]
                    u += 1

                    def fa(h=h, q_=q_, k_=k_, i=i, j=j, r=r, c0=c0, sp=sp, p_=p_):
                        if i == 0 and j == 0:
                            S.dma("sync", lambda e: e.dma_start(out=q_.ap, in_=self.QA.ap[h]), reads=[self.QA.b], writes=[q_.b])
                            S.dma("sync", lambda e: e.dma_start(out=k_.ap, in_=self.KA.ap[h]), reads=[self.KA.b], writes=[k_.b])
                        S.op("tensor", lambda e: e.matmul(out=sp.ap[:, c0:TS], lhsT=k_.ap[0:70, j * 128:(j + 1) * 128], rhs=q_.ap[0:70, i * TS + c0:(i + 1) * TS],
                                                          start=True, stop=True), reads=[k_.b, q_.b], writes=[sp.b])
                        if r >= 0:
                            S.op("vector", lambda e: e.tensor_tensor(out=sp.ap[:, c0:c0 + 128], in0=sp.ap[:, c0:c0 + 128], in1=self.trineg.ap, op=ALU.add),
                                 reads=[sp.b, self.trineg.b], writes=[sp.b])
                        S.op("scalar", lambda e: e.activation(out=p_.ap[:, c0:TS], in_=sp.ap[:, c0:TS], func=AF.Exp), reads=[sp.b], writes=[p_.b])

                    def fb(h=h, i=i, j=j, nj=nj, c0=c0, p_=p_, a=a, rd_=rd_, rd2_=rd2_, os_=os_, vc0=vc0, nrow=nrow, drow=drow):
                        S.op("tensor", lambda e: e.matmul(out=a.ap[:, c0:TS], lhsT=va.ap[:, j, vc0:vc0 + 128], rhs=p_.ap[:, c0:TS], start=(j == 0), stop=(j == nj - 1)),
                             reads=[va.b, p_.b], writes=[a.b])
                        if j == nj - 1:
                            S.op("vector", lambda e: e.reciprocal(out=rd_.ap[drow, :], in_=a.ap[drow, :]), reads=[a.b], writes=[rd_.b])
                            S.dma("sync", lambda e: e.dma_start(out=rd2_.ap[nrow, :], in_=rd_.ap[drow, :]), reads=[rd_.b], writes=[rd2_.b])
                            S.op("vector", lambda e: e.tensor_tensor(out=os_.ap[nrow, :], in0=a.ap[nrow, :], in1=rd2_.ap[nrow, :], op=ALU.mult),
                                 reads=[a.b, rd2_.b], writes=[os_.b])
                            S.dma("sync", lambda e: e.dma_start(out=self.MIX.ap[h * 64:(h + 1) * 64, i * TS:(i + 1) * TS], in_=os_.ap[nrow, :]),
                                  reads=[os_.b], writes=[self.MIX.b], accum=True)
                    A.append(fa)
                    B.append(fb)
        LA = 3
        for idx in range(len(A) + LA):
            if idx < len(A):
                A[idx]()
            if idx >= LA:
                B[idx - LA]()
        S.barrier()

    def phase_diff(self):
        S = self.S
        ar = self.ar
        ar.reset()
        dv = ar.alloc([128, NB, 512], BF16, "dv")
        S.dma("sync", lambda e: e.dma_start(out=dv.ap, in_=self.DV.ap.rearrange("(b p) c -> p b c", p=128)), reads=[self.DV.b], writes=[dv.b])
        dq = [ar.alloc([128, T], BF16, f"dq{k}") for k in range(2)]
        dk = [ar.alloc([128, T], BF16, f"dk{k}") for k in range(2)]
        pt = [ar.alloc([128, TS], BF16, f"dpt{k}") for k in range(6)]
        f1 = ar.alloc([128, TS], F32, "f1")
        f2 = ar.alloc([128, TS], F32, "f2")
        f3 = ar.alloc([128, TS], F32, "f3")
        sqb = ar.alloc([128, TS], BF16, "sqb")
        ost = [ar.alloc([128, TS], BF16, f"dost{k}") for k in range(2)]
        lam = ar.alloc([128, 4, 64], F32, "lam")
        lamt = ar.alloc([128, 2, 64], F32, "lamt")
        lams = ar.alloc([128, 8], F32, "lams")
        sg = ar.alloc([128, 1], F32, "sg")
        S.dma("sync", lambda e: e.dma_start(out=lam.ap, in_=self.lam_in.rearrange("(o a) d -> o a d", o=1).to_broadcast([128, 4, 64])), writes=[lam.b])
        S.op("vector", lambda e: e.tensor_tensor(out=lamt.ap[:, 0, :], in0=lam.ap[:, 0, :], in1=lam.ap[:, 1, :], op=ALU.mult), reads=[lam.b], writes=[lamt.b])
        S.op("vector", lambda e: e.tensor_tensor(out=lamt.ap[:, 1, :], in0=lam.ap[:, 2, :], in1=lam.ap[:, 3, :], op=ALU.mult), reads=[lam.b], writes=[lamt.b])
        S.op("vector", lambda e: e.reduce_sum(out=lams.ap[:, 0:2], in_=lamt.ap, axis=mybir.AxisListType.X), reads=[lamt.b], writes=[lams.b])
        S.op("scalar", lambda e: e.activation(out=lams.ap[:, 2:4], in_=lams.ap[:, 0:2], func=AF.Exp), reads=[lams.b], writes=[lams.b])
        S.op("vector", lambda e: e.tensor_tensor(out=lams.ap[:, 4:5], in0=lams.ap[:, 3:4], in1=lams.ap[:, 2:3], op=ALU.subtract), reads=[lams.b], writes=[lams.b])
        S.op("vector", lambda e: e.tensor_scalar(out=lams.ap[:, 5:6], in0=lams.ap[:, 4:5], scalar1=-0.2, scalar2=None, op0=ALU.add), reads=[lams.b], writes=[lams.b])
        neglam = lams.ap[:, 5:6]
        S.dma("sync", lambda e: e.dma_start(out=sg.ap, in_=self.subln), writes=[sg.b])
        S.op("vector", lambda e: e.tensor_scalar(out=sg.ap, in0=sg.ap, scalar1=0.8, scalar2=None, op0=ALU.mult), reads=[sg.b], writes=[sg.b])
        num = [self.ps[0], self.ps[1]]
        den = [self.ps[2], self.ps[3]]
        sps = self.ps[4:8]
        A, B = [], []
        u = 0
        tcount = 0
        V = "vector"
        for h in range(4):
            q_ = dq[h % 2]
            k_ = dk[h % 2]
            for i in range(NT):
                nj = 4 * i + 4
                os_ = ost[tcount % 2]
                tcount += 1
                for j in range(nj):
                    r = j - 4 * i
                    c0 = 128 * r if r > 0 else 0
                    for m in range(2):
                        sp = sps[u % 4]
                        p_ = pt[u % len(pt)]
                        u += 1
                        mr = slice(m * 64, (m + 1) * 64)

                        def fa(h=h, q_=q_, k_=k_, i=i, j=j, m=m, r=r, c0=c0, sp=sp, p_=p_, mr=mr):
                            if i == 0 and j == 0 and m == 0:
                                S.dma("sync", lambda e: e.dma_start(out=q_.ap, in_=self.DQ.ap[h]), reads=[self.DQ.b], writes=[q_.b])
                                S.dma("sync", lambda e: e.dma_start(out=k_.ap, in_=self.DK.ap[h]), reads=[self.DK.b], writes=[k_.b])
                            S.op("tensor", lambda e: e.matmul(out=sp.ap[:, c0:TS], lhsT=k_.ap[mr, j * 128:(j + 1) * 128], rhs=q_.ap[mr, i * TS + c0:(i + 1) * TS],
                                                              start=True, stop=True), reads=[k_.b, q_.b], writes=[sp.b])
                            if r >= 0:
                                S.op("vector", lambda e: e.tensor_tensor(out=sp.ap[:, c0:c0 + 128], in0=sp.ap[:, c0:c0 + 128], in1=self.dneg.ap, op=ALU.add),
                                     reads=[sp.b, self.dneg.b], writes=[sp.b])
                            S.op("scalar", lambda e: e.activation(out=p_.ap[:, c0:TS], in_=sp.ap[:, c0:TS], func=AF.Exp), reads=[sp.b], writes=[p_.b])

                        def fb(h=h, i=i, j=j, m=m, nj=nj, c0=c0, p_=p_, os_=os_):
                            S.op("tensor", lambda e: e.matmul(out=num[m].ap[:, c0:TS], lhsT=dv.ap[:, j, h * 128:(h + 1) * 128], rhs=p_.ap[:, c0:TS], start=(j == 0), stop=(j == nj - 1)),
                                 reads=[dv.b, p_.b], writes=[num[m].b])
                            S.op("tensor", lambda e: e.matmul(out=den[m].ap[:, c0:TS], lhsT=self.onesb.ap, rhs=p_.ap[:, c0:TS], start=(j == 0), stop=(j == nj - 1)),
                                 reads=[self.onesb.b, p_.b], writes=[den[m].b])
                            if j == nj - 1 and m == 1:
                                S.op(V, lambda e: e.reciprocal(out=f1.ap, in_=den[0].ap), reads=[den[0].b], writes=[f1.b])
                                S.op(V, lambda e: e.reciprocal(out=f2.ap, in_=den[1].ap), reads=[den[1].b], writes=[f2.b])
                                S.op(V, lambda e: e.tensor_tensor(out=f1.ap, in0=num[0].ap, in1=f1.ap, op=ALU.mult), reads=[num[0].b, f1.b], writes=[f1.b])
                                S.op(V, lambda e: e.tensor_tensor(out=f2.ap, in0=num[1].ap, in1=f2.ap, op=ALU.mult), reads=[num[1].b, f2.b], writes=[f2.b])
                                S.op(V, lambda e: e.scalar_tensor_tensor(out=f3.ap, in0=f2.ap, scalar=neglam, in1=f1.ap, op0=ALU.mult, op1=ALU.add),
                                     reads=[f1.b, f2.b], sreads=[lams.b], writes=[f3.b])
                                S.op("scalar", lambda e: e.activation(out=sqb.ap, in_=f3.ap, func=AF.Square), reads=[f3.b], writes=[sqb.b])
                                pm = self.ps[4 + (tcount_box[0] % 4)]
                                tcount_box[0] += 1
                                S.op("tensor", lambda e: e.matmul(out=pm.ap, lhsT=self.onesb.ap, rhs=sqb.ap, start=True, stop=True),
                                     reads=[self.onesb.b, sqb.b], writes=[pm.b])
                                S.op("scalar", lambda e: e.activation(out=f1.ap, in_=pm.ap, func=AF.Sqrt, bias=self.epsb.ap[:, 0:1], scale=1.0 / 128),
                                     reads=[pm.b, self.epsb.b], writes=[f1.b])
                                S.op(V, lambda e: e.reciprocal(out=f1.ap, in_=f1.ap), reads=[f1.b], writes=[f1.b])
                                S.op(V, lambda e: e.scalar_tensor_tensor(out=os_.ap, in0=f3.ap, scalar=sg.ap[:, 0:1], in1=f1.ap, op0=ALU.mult, op1=ALU.mult),
                                     reads=[f3.b, f1.b], sreads=[sg.b], writes=[os_.b])
                                S.dma("sync", lambda e: e.dma_start(out=self.MIX.ap[512 + h * 128:512 + (h + 1) * 128, i * TS:(i + 1) * TS], in_=os_.ap),
                                      reads=[os_.b], writes=[self.MIX.b], accum=True)
                        A.append(fa)
                        B.append(fb)
        tcount_box = [0]
        LA = 3
        for idx in range(len(A) + LA):
            if idx < len(A):
                A[idx]()
            if idx >= LA:
                B[idx - LA]()
        S.barrier()

    def phase_outproj_xattn(self, layer, w_out_dram):
        S = self.S
        ar = self.ar
        ar.reset()
        mkT = ar.alloc([128, 8, 256], BF16, "mkT")
        mv = ar.alloc([128, 2, D], BF16, "mv")
        mark = ar.off
        wkv = ar.alloc([128, 8, 2048], BF16, "wkv")
        self.load_w(wkv, self.xattn_wkv[layer])
        memt = ar.alloc([128, 2, D], F32, "memt")
        memn = ar.alloc([128, 2, D], BF16, "memn")
        mscr = ar.alloc([128, D], F32, "mscr")
        mss = ar.alloc([128, 4], F32, "mss")
        gmem = ar.alloc([128, D], F32, "gmem")
        memnT = ar.alloc([128, 8, 256], BF16, "memnT")
        S.dma("sync", lambda e: e.dma_start(out=memt.ap, in_=self.mem_in.rearrange("(s p) d -> p s d", p=128)), writes=[memt.b])
        S.dma("sync", lambda e: e.dma_start(out=gmem.ap, in_=self.memgain[layer:layer + 1, :].to_broadcast([128, D])), writes=[gmem.b])
        for s_ in range(2):
            S.op("vector", lambda e, s_=s_: e.tensor_tensor(out=mscr.ap, in0=memt.ap[:, s_, :], in1=memt.ap[:, s_, :], op=ALU.mult), reads=[memt.b], writes=[mscr.b])
            S.op("vector", lambda e, s_=s_: e.reduce_sum(out=mss.ap[:, s_:s_ + 1], in_=mscr.ap, axis=mybir.AxisListType.X), reads=[mscr.b], writes=[mss.b])
        S.op("scalar", lambda e: e.activation(out=mss.ap[:, 2:4], in_=mss.ap[:, 0:2], func=AF.Sqrt, bias=self.epsb.ap[:, 0:1], scale=1.0 / D),
             reads=[mss.b, self.epsb.b], writes=[mss.b])
        S.op("vector", lambda e: e.reciprocal(out=mss.ap[:, 2:4], in_=mss.ap[:, 2:4]), reads=[mss.b], writes=[mss.b])
        for s_ in range(2):
            S.op("vector", lambda e, s_=s_: e.scalar_tensor_tensor(out=memn.ap[:, s_, :], in0=memt.ap[:, s_, :], scalar=mss.ap[:, 2 + s_:3 + s_], in1=gmem.ap,
                                                                   op0=ALU.mult, op1=ALU.mult), reads=[memt.b, gmem.b], sreads=[mss.b], writes=[memn.b])
        ev = 0
        for c in range(8):
            p = self.psum()
            pbv = p.ap.bitcast(BF16)
            for s_ in range(2):
                S.op("tensor", lambda e, pbv=pbv, s_=s_, c=c: e.transpose(out=pbv[:, s_ * 128:(s_ + 1) * 128], in_=memn.ap[:, s_, c * 128:(c + 1) * 128], identity=self.identb.ap),
                     reads=[memn.b, self.identb.b], writes=[p.b])
            self.evac(ev, memnT.ap[:, c, :], pbv[:, 0:256], [p.b], [memnT.b]); ev += 1
        for c in range(8):
            p = self.psum()
            self.proj_fm(p, wkv, c * 128, 128, memnT, N=256)
            self.evac(ev, mkT.ap[:, c, :], p.ap[:, 0:256], [p.b], [mkT.b]); ev += 1
        for s_ in range(2):
            for half in range(2):
                p = self.psum()
                self.proj_tm(p, memnT, s_ * 128, wkv, 1024 + half * 512, 512)
                self.evac(ev, mv.ap[:, s_, half * 512:(half + 1) * 512], p.ap, [p.b], [mv.b]); ev += 1
        S.barrier()
        ar.off = mark
        w_out = ar.alloc([128, 8, D], BF16, "w_out")
        wq = ar.alloc([128, 8, D], BF16, "wq")
        wo = ar.alloc([128, 8, D], BF16, "wo")
        self.load_w(w_out, w_out_dram)
        self.load_w(wq, self.xattn_wq[layer])
        self.load_w(wo, self.xattn_wo[layer])
        mix = [ar.alloc([128, 8, TS], BF16, f"mix{k}") for k in range(2)]
        xt = [ar.alloc([128, 8, TS], F32, f"xxt{k}") for k in range(2)]
        xn = ar.alloc([128, 8, TS], BF16, "xxn")
        sq = ar.alloc([128, 8, TS], BF16, "xsq")
        rs = ar.alloc([128, TS], F32, "xrs")
        qx = ar.alloc([128, 8, TS], BF16, "qx")
        att = ar.alloc([128, 8, TS], BF16, "att")
        pt = [ar.alloc([128, 2, TS], BF16, f"xpt{k}") for k in range(2)]
        rd = [ar.alloc([128, TS], F32, f"xrd{k}") for k in range(2)]
        mixv = self.MIX.ap.rearrange("(c p) t -> p c t", p=128)
        for i in range(NT):
            m_ = mix[i % 2]
            x_ = xt[i % 2]
            S.dma("sync", lambda e, i=i, m_=m_: e.dma_start(out=m_.ap, in_=mixv[:, :, i * TS:(i + 1) * TS]), reads=[self.MIX.b], writes=[m_.b])
            S.dma("sync", lambda e, i=i, x_=x_: e.dma_start(out=x_.ap, in_=self.xt_tile_ap(i)), reads=[self.XT.b], writes=[x_.b])
            for jo in range(8):
                p = self.psum()
                self.proj_fm(p, w_out, jo * 128, 128, m_)
                S.op("vector", lambda e, p=p, jo=jo, x_=x_: e.tensor_tensor(out=x_.ap[:, jo, :], in0=p.ap, in1=x_.ap[:, jo, :], op=ALU.add),
                     reads=[p.b, x_.b], writes=[x_.b])
            if self.stop_after == f"mix{layer}":
                S.dma("sync", lambda e, i=i, x_=x_: e.dma_start(out=self.xt_tile_ap(i), in_=x_.ap), reads=[x_.b], writes=[self.XT.b], accum=True)
                continue
            self.rmsnorm(x_, xn, sq, rs, 2 + layer)
            for jo in range(8):
                p = self.psum()
                self.proj_fm(p, wq, jo * 128, 128, xn)
                self.evac(jo, qx.ap[:, jo, :], p.ap, [p.b], [qx.b], scale=1.0 / 16)
            for h in range(4):
                pt_ = pt[h % 2]
                rd_ = rd[h % 2]
                for mb in range(2):
                    p = self.psum()
                    for dc in range(2):
                        S.op("tensor", lambda e, p=p, h=h, dc=dc, mb=mb: e.matmul(out=p.ap, lhsT=mkT.ap[:, 2 * h + dc, mb * 128:(mb + 1) * 128], rhs=qx.ap[:, 2 * h + dc, :],
                                                                                  start=(dc == 0), stop=(dc == 1)), reads=[mkT.b, qx.b], writes=[p.b])
                    S.op("scalar", lambda e, p=p, pt_=pt_, mb=mb: e.activation(out=pt_.ap[:, mb, :], in_=p.ap, func=AF.Exp), reads=[p.b], writes=[pt_.b])
                pd = self.psum()
                for mb in range(2):
                    S.op("tensor", lambda e, pd=pd, pt_=pt_, mb=mb: e.matmul(out=pd.ap, lhsT=self.onesb.ap, rhs=pt_.ap[:, mb, :], start=(mb == 0), stop=(mb == 1)),
                         reads=[self.onesb.b, pt_.b], writes=[pd.b])
                S.op("vector", lambda e, pd=pd, rd_=rd_: e.reciprocal(out=rd_.ap, in_=pd.ap), reads=[pd.b], writes=[rd_.b])
                for ec in range(2):
                    p = self.psum()
                    for mb in range(2):
                        S.op("tensor", lambda e, p=p, pt_=pt_, mb=mb, h=h, ec=ec: e.matmul(out=p.ap, lhsT=mv.ap[:, mb, h * 256 + ec * 128:h * 256 + (ec + 1) * 128], rhs=pt_.ap[:, mb, :],
                                                                                         start=(mb == 0), stop=(mb == 1)), reads=[mv.b, pt_.b], writes=[p.b])
                    S.op("vector", lambda e, p=p, rd_=rd_, h=h, ec=ec: e.tensor_tensor(out=att.ap[:, 2 * h + ec, :], in0=p.ap, in1=rd_.ap, op=ALU.mult),
                         reads=[p.b, rd_.b], writes=[att.b])
            for jo in range(8):
                p = self.psum()
                self.proj_fm(p, wo, jo * 128, 128, att)
                S.op("vector", lambda e, p=p, jo=jo, x_=x_: e.tensor_tensor(out=x_.ap[:, jo, :], in0=p.ap, in1=x_.ap[:, jo, :], op=ALU.add),
                     reads=[p.b, x_.b], writes=[x_.b])
            S.dma("sync", lambda e, i=i, x_=x_: e.dma_start(out=self.xt_tile_ap(i), in_=x_.ap), reads=[x_.b], writes=[self.XT.b], accum=True)
        S.barrier()

    def phase_ffn(self, layer, final=False):
        S = self.S
        ar = self.ar
        ar.reset()
        w_up = ar.alloc([128, 8, 2 * DFF], BF16, "w_up")
        w_dn = ar.alloc([128, NH, D], BF16, "w_dn")
        self.load_w(w_up, self.ffn_w_up[layer])
        self.load_w(w_dn, self.ffn_w_down[layer])
        cw = ar.alloc([128, 3, 44], F32, "cw")
        cb = ar.alloc([128, 44], F32, "cb")
        S.dma("sync", lambda e: e.dma_start(out=cw.ap, in_=self.ffn_conv_w[layer]), writes=[cw.b])
        S.dma("sync", lambda e: e.dma_start(out=cb.ap, in_=self.ffn_conv_b[layer]), writes=[cb.b])
        xt = ar.alloc([128, 8, TS], F32, "fxt")
        xn = ar.alloc([128, 8, TS], BF16, "fxn")
        rs = ar.alloc([128, TS], F32, "frs")
        araw = ar.raw[:, ar.off:ar.off + NH * TS // 2]
        ar.off += NH * TS // 2
        act = TT(araw.bitcast(BF16).rearrange("p (a b) -> p a b", a=NH), "fact")
        sq = TT(act.ap[:, 0:8, :], "fsq")
        sq.b = act.b
        hg = [ar.alloc([128, TS + 2], F32, f"hg{k}") for k in range(2)]
        hu = [ar.alloc([128, TS + 2], F32, f"hu{k}") for k in range(2)]
        yg = [ar.alloc([128, TS], F32, f"yg{k}") for k in range(2)]
        yu = [ar.alloc([128, TS], F32, f"yu{k}") for k in range(2)]
        halo = ar.alloc([128, 44, 2], F32, "halo")
        S.op("gpsimd", lambda e: e.memset(halo.ap, 0.0), writes=[halo.b])
        if final:
            otm = [TT(araw[:, 2048 + k * 1024:2048 + (k + 1) * 1024], f"otm{k}") for k in range(2)]
            for o_ in otm:
                o_.b = act.b
        u = 0
        for i in range(NT):
            S.dma("sync", lambda e, i=i: e.dma_start(out=xt.ap, in_=self.xt_tile_ap(i)), reads=[self.XT.b], writes=[xt.b])
            self.rmsnorm(xt, xn, sq, rs, 6 + layer)
            for j in range(NH):
                hbs = [hg[u % 2], hu[u % 2]]
                ybs = [yg[u % 2], yu[u % 2]]
                u += 1
                for br in range(2):
                    cidx = br * NH + j
                    hb = hbs[br]
                    yb = ybs[br]
                    p = self.psum()
                    self.proj_fm(p, w_up, br * DFF + j * 128, 128, xn)
                    S.op("scalar", lambda e, hb=hb, p=p: e.copy(out=hb.ap[:, 2:TS + 2], in_=p.ap), reads=[p.b], writes=[hb.b])
                    S.op("scalar", lambda e, yb=yb, p=p, cidx=cidx: e.activation(out=yb.ap, in_=p.ap, func=AF.Identity, bias=cb.ap[:, cidx:cidx + 1], scale=cw.ap[:, 2, cidx:cidx + 1]),
                         reads=[p.b, cw.b, cb.b], writes=[yb.b])
                    S.op("gpsimd", lambda e, hb=hb, cidx=cidx: e.tensor_copy(out=hb.ap[:, 0:2], in_=halo.ap[:, cidx, :]), reads=[halo.b], writes=[hb.b])
                    S.op("gpsimd", lambda e, hb=hb, cidx=cidx: e.tensor_copy(out=halo.ap[:, cidx, :], in_=hb.ap[:, TS:TS + 2]), reads=[hb.b], writes=[halo.b])
                for tap in (1, 0):
                    for br in range(2):
                        cidx = br * NH + j
                        hb = hbs[br]
                        yb = ybs[br]
                        S.op("vector", lambda e, hb=hb, yb=yb, cidx=cidx, tap=tap: e.scalar_tensor_tensor(out=yb.ap, in0=hb.ap[:, tap:tap + TS], scalar=cw.ap[:, tap, cidx:cidx + 1], in1=yb.ap,
                                                                                                 op0=ALU.mult, op1=ALU.add), reads=[hb.b, cw.b, yb.b], writes=[yb.b])
                g_ = ybs[0]
                u_ = ybs[1]
                S.op("scalar", lambda e, g_=g_: e.activation(out=g_.ap, in_=g_.ap, func=AF.Gelu_apprx_tanh), reads=[g_.b], writes=[g_.b])
                S.op("gpsimd", lambda e, g_=g_, u_=u_, j=j: e.tensor_tensor(out=act.ap[:, j, :], in0=g_.ap, in1=u_.ap, op=ALU.mult), reads=[g_.b, u_.b], writes=[act.b])
            for half in range(2):
                accs = [self.psum() for _ in range(4)]
                for j in range(NH):
                    for q4 in range(4):
                        jo = half * 4 + q4
                        p = accs[q4]
                        S.op("tensor", lambda e, p=p, j=j, jo=jo: e.matmul(out=p.ap, lhsT=w_dn.ap[:, j, jo * 128:(jo + 1) * 128], rhs=act.ap[:, j, :], start=(j == 0), stop=(j == NH - 1)),
                             reads=[w_dn.b, act.b], writes=[p.b])
                for q4 in range(4):
                    jo = half * 4 + q4
                    p = accs[q4]
                    S.op("vector", lambda e, p=p, jo=jo: e.tensor_tensor(out=xt.ap[:, jo, :], in0=p.ap, in1=xt.ap[:, jo, :], op=ALU.add), reads=[p.b, xt.b], writes=[xt.b])
            if not final:
                S.dma("sync", lambda e, i=i: e.dma_start(out=self.xt_tile_ap(i), in_=xt.ap), reads=[xt.b], writes=[self.XT.b], accum=True)
            else:
                self.final_out(i, xt, sq, rs, otm)
        S.barrier()

    def phase_ffn2(self, layer, final=False):
        S = self.S
        ar = self.ar
        ar.reset()
        TF = 256
        NTF = T // TF
        w_up = ar.alloc([128, 8, 2 * DFF], BF16, "w_up")
        w_dn = ar.alloc([128, NH, D], BF16, "w_dn")
        self.load_w(w_up, self.ffn_w_up[layer])
        self.load_w(w_dn, self.ffn_w_down[layer])
        cw = ar.alloc([128, 3, 44], F32, "cw")
        cb = ar.alloc([128, 44], F32, "cb")
        S.dma("sync", lambda e: e.dma_start(out=cw.ap, in_=self.ffn_conv_w[layer]), writes=[cw.b])
        S.dma("sync", lambda e: e.dma_start(out=cb.ap, in_=self.ffn_conv_b[layer]), writes=[cb.b])
        xt = [ar.alloc([128, 8, TF], F32, f"fxt{k}") for k in range(2)]
        xn = [ar.alloc([128, 8, TF], BF16, f"fxn{k}") for k in range(2)]
        sq = ar.alloc([128, 8, TF], BF16, "fsq")
        rs = [ar.alloc([128, TF], F32, f"frs{k}") for k in range(2)]
        NR = 4
        hb = [[ar.alloc([128, TF + 2], F32, f"hb{k}_{br}") for br in range(2)] for k in range(NR)]
        yb = [[ar.alloc([128, TF], F32, f"yb{k}_{br}") for br in range(2)] for k in range(NR)]
        actb = [ar.alloc([128, TF], BF16, f"actb{k}") for k in range(NR)]
        halo = ar.alloc([128, 44, 2], F32, "halo")
        S.op("gpsimd", lambda e: e.memset(halo.ap, 0.0), writes=[halo.b])
        if final:
            otm = [ar.alloc([128, D], F32, f"otm{k}") for k in range(2)]
            fsq = ar.alloc([128, 8, TF], BF16, "ffsq")
            frs = ar.alloc([128, TF], F32, "ffrs")
        accb = self.ps[0:4]
        ub = self.ps[4:7]
        nb = self.ps[7]
        xv = self.XT.ap.rearrange("(c p) t -> p c t", p=128)

        def load_norm(t):
            x_ = xt[t % 2]
            S.dma("sync", lambda e: e.dma_start(out=x_.ap, in_=xv[:, :, t * TF:(t + 1) * TF]), reads=[self.XT.b], writes=[x_.b])
            self.rmsnorm(x_, xn[t % 2], sq, rs[t % 2], 6 + layer, p=nb)

        def down(t, j):
            a_ = actb[j % NR]
            for jo in range(8):
                bank = accb[jo // 2]
                cs_ = (jo % 2) * TF
                st = (j == 0 and jo % 2 == 0)
                S.op("tensor", lambda e, bank=bank, cs_=cs_, jo=jo, st=st, a_=a_, j=j: e.matmul(out=bank.ap[:, cs_:cs_ + TF], lhsT=w_dn.ap[:, j, jo * 128:(jo + 1) * 128], rhs=a_.ap,
                                                                                       start=st, stop=(j == NH - 1), skip_group_check=True),
                     reads=[w_dn.b, a_.b], writes=[bank.b])

        load_norm(0)
        for t in range(NTF):
            x_ = xt[t % 2]
            xn_ = xn[t % 2]
            if t + 1 < NTF:
                load_norm(t + 1)
            for j in range(NH):
                U = ub[j % 3]
                k = j % NR
                for br in range(2):
                    cidx = br * NH + j
                    h_ = hb[k][br]
                    y_ = yb[k][br]
                    col = br * TF
                    for kc in range(8):
                        S.op("tensor", lambda e, kc=kc, U=U, col=col, br=br, j=j, xn_=xn_: e.matmul(out=U.ap[:, col:col + TF], lhsT=w_up.ap[:, kc, br * DFF + j * 128:br * DFF + (j + 1) * 128],
                                                                                  rhs=xn_.ap[:, kc, :], start=(kc == 0), stop=(kc == 7)),
                             reads=[w_up.b, xn_.b], writes=[U.b])
                    S.op("scalar", lambda e, h_=h_, U=U, col=col: e.copy(out=h_.ap[:, 2:TF + 2], in_=U.ap[:, col:col + TF]), reads=[U.b], writes=[h_.b])
                    S.op("gpsimd", lambda e, h_=h_, cidx=cidx: e.tensor_copy(out=h_.ap[:, 0:2], in_=halo.ap[:, cidx, :]), reads=[halo.b], writes=[h_.b])
                    S.op("gpsimd", lambda e, h_=h_, cidx=cidx: e.tensor_copy(out=halo.ap[:, cidx, :], in_=h_.ap[:, TF:TF + 2]), reads=[h_.b], writes=[halo.b])
                    S.op("gpsimd", lambda e, h_=h_, y_=y_, cidx=cidx: e.tensor_scalar(out=y_.ap, in0=h_.ap[:, 2:TF + 2], scalar1=cw.ap[:, 2, cidx:cidx + 1], scalar2=cb.ap[:, cidx:cidx + 1],
                                                                                     op0=ALU.mult, op1=ALU.add), reads=[h_.b, cw.b, cb.b], writes=[y_.b])
                for tap in (1, 0):
                    for br in range(2):
                        cidx = br * NH + j
                        h_ = hb[k][br]
                        y_ = yb[k][br]
                        S.op("vector", lambda e, h_=h_, y_=y_, cidx=cidx, tap=tap: e.scalar_tensor_tensor(out=y_.ap, in0=h_.ap[:, tap:tap + TF], scalar=cw.ap[:, tap, cidx:cidx + 1], in1=y_.ap,
                                                                                                 op0=ALU.mult, op1=ALU.add), reads=[h_.b, cw.b, y_.b], writes=[y_.b])
                g_ = yb[k][0]
                u_ = yb[k][1]
                a_ = actb[k]
                S.op("scalar", lambda e, g_=g_: e.activation(out=g_.ap, in_=g_.ap, func=AF.Gelu_apprx_tanh), reads=[g_.b], writes=[g_.b])
                S.op("gpsimd", lambda e, g_=g_, u_=u_, a_=a_: e.tensor_tensor(out=a_.ap, in0=g_.ap, in1=u_.ap, op=ALU.mult), reads=[g_.b, u_.b], writes=[a_.b])
                if j >= 2:
                    down(t, j - 2)
            down(t, NH - 2)
            down(t, NH - 1)
            for jo in range(8):
                bank = accb[jo // 2]
                cs_ = (jo % 2) * TF
                S.op("vector", lambda e, bank=bank, cs_=cs_, jo=jo, x_=x_: e.tensor_tensor(out=x_.ap[:, jo, :], in0=bank.ap[:, cs_:cs_ + TF], in1=x_.ap[:, jo, :], op=ALU.add),
                     reads=[bank.b, x_.b], writes=[x_.b])
            if not final:
                S.dma("sync", lambda e, t=t, x_=x_: e.dma_start(out=xv[:, :, t * TF:(t + 1) * TF], in_=x_.ap), reads=[x_.b], writes=[self.XT.b], accum=True)
            else:
                self.final_out(t, x_, fsq, frs, otm, p=nb, TN=TF)
        S.barrier()

    def final_out(self, i, xt, sq, rs, otm, p=None, TN=TS):
        S = self.S
        S.op("scalar", lambda e: e.activation(out=sq.ap, in_=xt.ap, func=AF.Square), reads=[xt.b], writes=[sq.b])
        p0 = p if p is not None else self.psum()
        for c in range(8):
            S.op("tensor", lambda e, c=c: e.matmul(out=p0.ap[:, 0:TN], lhsT=self.onesb.ap, rhs=sq.ap[:, c, :], start=(c == 0), stop=(c == 7)),
                 reads=[self.onesb.b, sq.b], writes=[p0.b])
        S.op("scalar", lambda e: e.activation(out=rs.ap, in_=p0.ap[:, 0:TN], func=AF.Sqrt, bias=self.epsb.ap[:, 0:1], scale=1.0 / D),
             reads=[p0.b, self.epsb.b], writes=[rs.b])
        S.op("vector", lambda e: e.reciprocal(out=rs.ap, in_=rs.ap), reads=[rs.b], writes=[rs.b])
        for c in range(8):
            S.op("vector", lambda e, c=c: e.scalar_tensor_tensor(out=xt.ap[:, c, :], in0=xt.ap[:, c, :], scalar=self.gain.ap[:, 8, c:c + 1], in1=rs.ap,
                                                                   op0=ALU.mult, op1=ALU.mult), reads=[xt.b, rs.b, self.gain.b], writes=[xt.b])
        ev = 0
        for s_ in range(TN // 128):
            o_ = otm[s_ % 2]
            for half in range(2):
                pp = p if p is not None else self.psum()
                for cc in range(4):
                    c = half * 4 + cc
                    S.op("tensor", lambda e, pp=pp, cc=cc, c=c, s_=s_: e.transpose(out=pp.ap[:, cc * 128:(cc + 1) * 128], in_=xt.ap[:, c, s_ * 128:(s_ + 1) * 128], identity=self.ident.ap),
                         reads=[xt.b, self.ident.b], writes=[pp.b])
                self.evac(ev, o_.ap[:, half * 512:(half + 1) * 512], pp.ap, [pp.b], [o_.b]); ev += 1
            S.dma("sync", lambda e, o_=o_, s_=s_: e.dma_start(out=self.out[i * TN + s_ * 128:i * TN + (s_ + 1) * 128, :], in_=o_.ap), reads=[o_.b], writes=[self.b_out], accum=True)

    def phase_l1_inproj(self):
        S = self.S
        ar = self.ar
        ar.reset()
        ipre = ar.alloc([4, T], F32, "ipre")
        lfn = ar.alloc([4, T], F32, "lfn1")
        self.m_ipre, self.m_lfn = ipre, lfn
        self.m_mark = ar.off
        w = ar.alloc([128, 8, 3080], BF16, "w_in1")
        self.load_w(w, self.mlstm_w_in)
        xt = [ar.alloc([128, 8, TS], F32, f"mxt{k}") for k in range(2)]
        xn = [ar.alloc([128, 8, TS], BF16, f"mxn{k}") for k in range(2)]
        sq = ar.alloc([128, 8, TS], BF16, "msq")
        rs = ar.alloc([128, TS], F32, "mrs")
        hb = [ar.alloc([128, TS + 3], F32, f"mhb{k}") for k in range(2)]
        yb = [ar.alloc([128, TS], F32, f"myb{k}") for k in range(2)]
        stg = [ar.alloc([128, TS], BF16, f"mstg{k}") for k in range(4)]
        vst = [ar.alloc([128, 4, 1028], BF16, f"mvst{k}") for k in range(2)]
        halo = ar.alloc([128, 8, 3], F32, "mhalo")
        mcw = ar.alloc([128, 4, 8], F32, "mcw")
        gb = ar.alloc([4, 2], F32, "gb")
        nbf = ar.alloc([4, 1], F32, "nbf1")
        tmp4 = ar.alloc([4, TS], F32, "tmp4")
        S.op("gpsimd", lambda e: e.memset(halo.ap, 0.0), writes=[halo.b])
        for k in range(2):
            S.op("gpsimd", lambda e, k=k: e.memset(vst[k].ap, 1.0), writes=[vst[k].b])
        S.dma("sync", lambda e: e.dma_start(out=mcw.ap, in_=self.mconv), writes=[mcw.b])
        S.dma("sync", lambda e: e.dma_start(out=gb.ap, in_=self.mgate_b), writes=[gb.b])
        S.op("vector", lambda e: e.tensor_scalar(out=nbf.ap, in0=gb.ap[:, 1:2], scalar1=-1.0, scalar2=None, op0=ALU.mult), reads=[gb.b], writes=[nbf.b])
        qscale = 128.0 ** -0.5
        u = 0
        si = 0
        ev = 0
        def prep1(i):
            x_ = xt[i % 2]
            S.dma("sync", lambda e: e.dma_start(out=x_.ap, in_=self.xt_tile_ap(i)), reads=[self.XT.b], writes=[x_.b])
            self.rmsnorm(x_, xn[i % 2], sq, rs, 1)

        prep1(0)
        for i in range(NT):
            x_ = xt[i % 2]
            xn_ = xn[i % 2]
            if i + 1 < NT:
                prep1(i + 1)
            for c in range(8):
                h_ = hb[u % 2]
                y_ = yb[u % 2]
                u += 1
                p = self.psum()
                self.proj_fm(p, w, c * 128, 128, xn_)
                S.op("scalar", lambda e, h_=h_, p=p: e.copy(out=h_.ap[:, 3:TS + 3], in_=p.ap), reads=[p.b], writes=[h_.b])
                S.op("gpsimd", lambda e, h_=h_, c=c: e.tensor_copy(out=h_.ap[:, 0:3], in_=halo.ap[:, c, :]), reads=[halo.b], writes=[h_.b])
                S.op("gpsimd", lambda e, h_=h_, c=c: e.tensor_copy(out=halo.ap[:, c, :], in_=h_.ap[:, TS:TS + 3]), reads=[h_.b], writes=[halo.b])
                S.op("vector", lambda e, h_=h_, y_=y_, c=c: e.tensor_scalar(out=y_.ap, in0=h_.ap[:, 3:TS + 3], scalar1=mcw.ap[:, 3, c:c + 1], scalar2=None, op0=ALU.mult),
                     reads=[h_.b, mcw.b], writes=[y_.b])
                for tap in range(3):
                    S.op("vector", lambda e, h_=h_, y_=y_, c=c, tap=tap: e.scalar_tensor_tensor(out=y_.ap, in0=h_.ap[:, tap:tap + TS], scalar=mcw.ap[:, tap, c:c + 1], in1=y_.ap,
                                                                                              op0=ALU.mult, op1=ALU.add), reads=[h_.b, mcw.b, y_.b], writes=[y_.b])
                st = stg[si % 4]; si += 1
                if c < 4:
                    S.op("scalar", lambda e, y_=y_: e.activation(out=y_.ap, in_=y_.ap, func=AF.Silu), reads=[y_.b], writes=[y_.b])
                    S.op("vector", lambda e, y_=y_, st=st: e.tensor_scalar(out=st.ap, in0=y_.ap, scalar1=qscale, scalar2=None, op0=ALU.mult), reads=[y_.b], writes=[st.b])
                else:
                    S.op("scalar", lambda e, y_=y_, st=st: e.activation(out=st.ap, in_=y_.ap, func=AF.Silu), reads=[y_.b], writes=[st.b])
                S.dma("sync", lambda e, st=st, c=c, i=i: e.dma_start(out=self.QK.ap[c, :, i * TS:(i + 1) * TS], in_=st.ap), reads=[st.b], writes=[self.QK.b], accum=True)
            vs = vst[i % 2]
            for s_ in range(4):
                for half in range(2):
                    p = self.psum()
                    self.proj_tm(p, xn_, s_ * 128, w, 1024 + half * 512, 512)
                    ov = vs.ap[:, s_, :].rearrange("p (h c) -> p h c", c=257)[:, 2 * half:2 * half + 2, 0:256]
                    pv = p.ap.rearrange("p (h c) -> p h c", c=256)
                    self.evac(ev, ov, pv, [p.b], [vs.b]); ev += 1
            S.dma("sync", lambda e, vs=vs, i=i: e.dma_start(out=self.VM.ap[i * TS:(i + 1) * TS, :].rearrange("(s p) c -> p s c", p=128), in_=vs.ap),
                  reads=[vs.b], writes=[self.VM.b], accum=True)
            p = self.psum()
            self.proj_fm(p, w, 2048, 4, xn_)
            S.op("scalar", lambda e, p=p, i=i: e.activation(out=ipre.ap[:, i * TS:(i + 1) * TS], in_=p.ap[0:4, :], func=AF.Identity, bias=gb.ap[:, 0:1], scale=1.0),
                 reads=[p.b, gb.b], writes=[ipre.b])
            p = self.psum()
            self.proj_fm(p, w, 2052, 4, xn_)
            S.op("scalar", lambda e, p=p: e.activation(out=tmp4.ap, in_=p.ap[0:4, :], func=AF.Exp, bias=nbf.ap[:, 0:1], scale=-1.0), reads=[p.b, nbf.b], writes=[tmp4.b])
            S.op("scalar", lambda e, i=i: e.activation(out=lfn.ap[:, i * TS:(i + 1) * TS], in_=tmp4.ap, func=AF.Ln, bias=1.0, scale=1.0), reads=[tmp4.b], writes=[lfn.b])
            for c in range(8):
                p = self.psum()
                self.proj_fm(p, w, 2056 + c * 128, 128, xn_)
                st = stg[si % 4]; si += 1
                S.op("scalar", lambda e, p=p, st=st: e.activation(out=st.ap, in_=p.ap, func=AF.Sigmoid), reads=[p.b], writes=[st.b])
                S.dma("sync", lambda e, st=st, c=c, i=i: e.dma_start(out=self.OG.ap[c * 128:(c + 1) * 128, i * TS:(i + 1) * TS], in_=st.ap), reads=[st.b], writes=[self.OG.b], accum=True)
        S.barrier()

    def phase_mlstm(self):
        S = self.S
        ar = self.ar
        ar.off = self.m_mark
        ipre, lfn = self.m_ipre, self.m_lfn
        V = "vector"
        negM = ar.alloc([4, T], F32, "mnegM")
        emarg = ar.alloc([4, T], F32, "memarg")
        a_tm = ar.alloc([128, NB, 4], F32, "a_tm")
        MEND = ar.alloc([128, 4, NB + 1], F32, "MEND")
        hn = ar.alloc([128, 8], F32, "hn")
        mark2 = ar.off
        cs = ar.alloc([4, T], F32, "mcs")
        a = ar.alloc([4, T], F32, "ma")
        Mx = ar.alloc([4, T], F32, "mMx")
        ones4 = ar.alloc([4, T], F32, "mones4")
        S.dma("sync", lambda e: e.dma_start(out=hn.ap, in_=self.mhn), writes=[hn.b])
        S.op("gpsimd", lambda e: e.memset(ones4.ap, 1.0), writes=[ones4.b])
        S.op("gpsimd", lambda e: e.memset(MEND.ap, 0.0), writes=[MEND.b])
        for i in range(NT):
            sl = slice(i * TS, (i + 1) * TS)
            init = 0.0 if i == 0 else cs.ap[:, i * TS - 1:i * TS]
            S.op(V, lambda e, sl=sl, init=init: e.tensor_tensor_scan(out=cs.ap[:, sl], data0=ones4.ap[:, sl], data1=lfn.ap[:, sl], initial=init, op0=ALU.mult, op1=ALU.add),
                 reads=[ones4.b, lfn.b], sreads=[cs.b], writes=[cs.b])
        S.op(V, lambda e: e.tensor_tensor(out=a.ap, in0=ipre.ap, in1=cs.ap, op=ALU.add), reads=[ipre.b, cs.b], writes=[a.b])
        for i in range(NT):
            sl = slice(i * TS, (i + 1) * TS)
            init = 0.0 if i == 0 else Mx.ap[:, i * TS - 1:i * TS]
            S.op(V, lambda e, sl=sl, init=init: e.tensor_tensor_scan(out=Mx.ap[:, sl], data0=ones4.ap[:, sl], data1=a.ap[:, sl], initial=init, op0=ALU.mult, op1=ALU.max),
                 reads=[ones4.b, a.b], sreads=[Mx.b], writes=[Mx.b])
        S.op(V, lambda e: e.tensor_scalar(out=negM.ap, in0=Mx.ap, scalar1=-1.0, scalar2=None, op0=ALU.mult), reads=[Mx.b], writes=[negM.b])
        S.op(V, lambda e: e.tensor_tensor(out=emarg.ap, in0=cs.ap, in1=Mx.ap, op=ALU.subtract), reads=[cs.b, Mx.b], writes=[emarg.b])
        for g in range(4):
            p = self.psum()
            for bb in range(8):
                b_ = g * 8 + bb
                S.op("tensor", lambda e, p=p, bb=bb, b_=b_: e.transpose(out=p.ap[:, bb * 4:(bb + 1) * 4], in_=a.ap[0:4, b_ * 128:(b_ + 1) * 128], identity=self.ident.ap[0:4, 0:4]),
                     reads=[a.b, self.ident.b], writes=[p.b])
            S.op(V, lambda e, p=p, g=g: e.tensor_copy(out=a_tm.ap[:, g * 8:(g + 1) * 8, :], in_=p.ap[:, 0:32].rearrange("p (b h) -> p b h", h=4)), reads=[p.b], writes=[a_tm.b])
        S.barrier()
        ar.off = mark2
        C32 = [ar.alloc([128, 257], F32, f"C32_{h}") for h in range(4)]
        Cbf = [ar.alloc([128, 256], BF16, f"Cbf_{h}") for h in range(4)]
        nrep = [ar.alloc([128, 128], BF16, f"nrep_{h}") for h in range(4)]
        for h in range(4):
            S.op("gpsimd", lambda e, h=h: e.memset(C32[h].ap, 0.0), writes=[C32[h].b])
            S.op("gpsimd", lambda e, h=h: e.memset(Cbf[h].ap, 0.0), writes=[Cbf[h].b])
            S.op("gpsimd", lambda e, h=h: e.memset(nrep[h].ap, 0.0), writes=[nrep[h].b])
        NMt = [ar.alloc([128, TS], F32, f"NMt{h}") for h in range(4)]
        EMt = [ar.alloc([128, TS], F32, f"EMt{h}") for h in range(4)]
        qT = [ar.alloc([128, TS], BF16, f"mqT{h}") for h in range(4)]
        kT = [ar.alloc([128, TS], BF16, f"mkT{h}") for h in range(4)]
        ogt = [ar.alloc([128, 2, TS], BF16, f"ogt{h}") for h in range(4)]
        hT = [ar.alloc([128, 2, TS], F32, f"hT{h}") for h in range(4)]
        vt = [ar.alloc([128, 4, 1028], BF16, f"mvt{k}") for k in range(2)]
        Wt = [ar.alloc([128, 128], F32, f"Wt{k}") for k in range(4)]
        Wm = [ar.alloc([128, 128], F32, f"Wm{k}") for k in range(4)]
        At = [ar.alloc([128, 128], BF16, f"At{k}") for k in range(4)]
        wint = [ar.alloc([128, 128], F32, f"wint{k}") for k in range(4)]
        qt = [ar.alloc([128, 128], BF16, f"qt{k}") for k in range(4)]
        kt = [ar.alloc([128, 128], BF16, f"kt{k}") for k in range(4)]
        wk = [ar.alloc([128, 2], F32, f"wk{k}") for k in range(4)]
        dd = [ar.alloc([128, 128], F32, f"dd{k}") for k in range(4)]
        sq2 = ar.alloc([128, 2, TS], BF16, "sq2")
        rs = ar.alloc([128, TS], F32, "mrs2")
        tmpf = ar.alloc([128, 2, TS], F32, "tmpf")
        ostg = [ar.alloc([128, 2, TS], BF16, f"mostg{k}") for k in range(2)]
        u = 0
        oc = 0
        for i in range(NT):
            sl = slice(i * TS, (i + 1) * TS)
            v_ = vt[i % 2]
            S.dma("sync", lambda e, v_=v_, i=i: e.dma_start(out=v_.ap, in_=self.VM.ap[i * TS:(i + 1) * TS, :].rearrange("(s p) c -> p s c", p=128)),
                  reads=[self.VM.b], writes=[v_.b])
            for h in range(4):
                S.dma("sync", lambda e, h=h, sl=sl: e.dma_start(out=qT[h].ap, in_=self.QK.ap[h, :, sl]), reads=[self.QK.b], writes=[qT[h].b])
                S.dma("sync", lambda e, h=h, sl=sl: e.dma_start(out=kT[h].ap, in_=self.QK.ap[4 + h, :, sl]), reads=[self.QK.b], writes=[kT[h].b])
                S.dma("sync", lambda e, h=h, sl=sl: e.dma_start(out=ogt[h].ap, in_=self.OG.ap[h * 256:(h + 1) * 256, sl].rearrange("(e p) t -> p e t", p=128)),
                      reads=[self.OG.b], writes=[ogt[h].b])
                p = self.psum()
                S.op("tensor", lambda e, p=p, h=h, sl=sl: e.matmul(out=p.ap, lhsT=self.sel.ap[0:4, h, :], rhs=negM.ap[0:4, sl], start=True, stop=True),
                     reads=[self.sel.b, negM.b], writes=[p.b])
                S.op("scalar", lambda e, p=p, h=h: e.copy(out=NMt[h].ap, in_=p.ap), reads=[p.b], writes=[NMt[h].b])
                S.op(V, lambda e, h=h, i=i: e.tensor_scalar(out=MEND.ap[:, h, 4 * i + 1:4 * i + 5], in0=NMt[h].ap.rearrange("p (c k) -> p c k", k=128)[:, :, 127],
                                                           scalar1=-1.0, scalar2=None, op0=ALU.mult), reads=[NMt[h].b], writes=[MEND.b])
                p = self.psum()
                S.op("tensor", lambda e, p=p, h=h, sl=sl: e.matmul(out=p.ap, lhsT=self.sel.ap[0:4, h, :], rhs=emarg.ap[0:4, sl], start=True, stop=True),
                     reads=[self.sel.b, emarg.b], writes=[p.b])
                S.op("scalar", lambda e, p=p, h=h: e.activation(out=EMt[h].ap, in_=p.ap, func=AF.Exp), reads=[p.b], writes=[EMt[h].b])
            for cc in range(4):
                c = 4 * i + cc
                cols = slice(cc * 128, (cc + 1) * 128)
                b1 = [None] * 4
                b2 = [None] * 4
                for h in range(4):
                    b1[h] = self.psum()
                    ptv = b1[h].ap.bitcast(BF16)
                    S.op("tensor", lambda e, bb=b1[h], h=h, cols=cols: e.matmul(out=bb.ap[:, 384:512], lhsT=kT[h].ap[:, cols], rhs=qT[h].ap[:, cols], start=True, stop=True),
                         reads=[kT[h].b, qT[h].b], writes=[b1[h].b])
                    S.op("tensor", lambda e, ptv=ptv, h=h, cols=cols: e.transpose(out=ptv[:, 520:648], in_=kT[h].ap[:, cols], identity=self.identb.ap),
                         reads=[kT[h].b, self.identb.b], writes=[b1[h].b])
                for h in range(4):
                    Mo = MEND.ap[:, h, c:c + 1]
                    negMn = NMt[h].ap[:, cc * 128 + 127:cc * 128 + 128]
                    acol = a_tm.ap[:, c, h:h + 1]
                    ptv = b1[h].ap.bitcast(BF16)
                    S.op("scalar", lambda e, h=h, cols=cols, acol=acol: e.activation(out=Wt[h].ap, in_=NMt[h].ap[:, cols], func=AF.Exp, bias=acol, scale=1.0),
                         reads=[NMt[h].b], sreads=[a_tm.b], writes=[Wt[h].b])
                    S.op("scalar", lambda e, h=h, cols=cols, Mo=Mo: e.activation(out=wint[h].ap, in_=NMt[h].ap[:, cols], func=AF.Exp, bias=Mo, scale=1.0),
                         reads=[NMt[h].b], sreads=[MEND.b], writes=[wint[h].b])
                    S.op("scalar", lambda e, h=h, acol=acol, negMn=negMn: e.activation(out=wk[h].ap[:, 0:1], in_=acol, func=AF.Exp, bias=negMn, scale=1.0),
                         reads=[a_tm.b], sreads=[NMt[h].b], writes=[wk[h].b])
                    S.op("scalar", lambda e, h=h, Mo=Mo, negMn=negMn: e.activation(out=wk[h].ap[:, 1:2], in_=Mo, func=AF.Exp, bias=negMn, scale=1.0),
                         reads=[MEND.b], sreads=[NMt[h].b], writes=[wk[h].b])
                    S.op("gpsimd", lambda e, h=h: e.tensor_tensor(out=Wm[h].ap, in0=Wt[h].ap, in1=self.tri01.ap, op=ALU.mult), reads=[Wt[h].b, self.tri01.b], writes=[Wm[h].b])
                    S.op("gpsimd", lambda e, h=h, cols=cols: e.tensor_tensor(out=qt[h].ap, in0=qT[h].ap[:, cols], in1=wint[h].ap, op=ALU.mult),
                         reads=[qT[h].b, wint[h].b], writes=[qt[h].b])
                    S.op(V, lambda e, h=h, bb=b1[h]: e.tensor_tensor(out=At[h].ap, in0=bb.ap[:, 384:512], in1=Wm[h].ap, op=ALU.mult), reads=[b1[h].b, Wm[h].b], writes=[At[h].b])
                    S.op(V, lambda e, h=h, ptv=ptv: e.tensor_scalar(out=kt[h].ap, in0=ptv[:, 520:648], scalar1=wk[h].ap[:, 0:1], scalar2=None, op0=ALU.mult),
                         reads=[b1[h].b], sreads=[wk[h].b], writes=[kt[h].b])
                for h in range(4):
                    b2[h] = self.psum()
                    ps_n = b2[h]
                    for e_ in range(2):
                        S.op("tensor", lambda e, ps_n=ps_n, e_=e_, h=h: e.matmul(out=ps_n.ap[:, e_ * 128:(e_ + 1) * 128], lhsT=Cbf[h].ap[:, e_ * 128:(e_ + 1) * 128], rhs=qt[h].ap,
                                                                          start=True, stop=False), reads=[Cbf[h].b, qt[h].b], writes=[ps_n.b])
                        S.op("tensor", lambda e, ps_n=ps_n, e_=e_, h=h, cc=cc, v_=v_: e.matmul(out=ps_n.ap[:, e_ * 128:(e_ + 1) * 128],
                                                                                        lhsT=v_.ap[:, cc, h * 257 + e_ * 128:h * 257 + (e_ + 1) * 128], rhs=At[h].ap,
                                                                                        start=False, stop=True), reads=[v_.b, At[h].b], writes=[ps_n.b])
                    S.op("tensor", lambda e, ps_n=ps_n, h=h: e.matmul(out=ps_n.ap[:, 256:384], lhsT=nrep[h].ap, rhs=qt[h].ap, start=True, stop=False),
                         reads=[nrep[h].b, qt[h].b], writes=[ps_n.b])
                    S.op("tensor", lambda e, ps_n=ps_n, h=h: e.matmul(out=ps_n.ap[:, 256:384], lhsT=self.onesb.ap, rhs=At[h].ap, start=False, stop=True),
                         reads=[self.onesb.b, At[h].b], writes=[ps_n.b])
                    S.op("tensor", lambda e, bb=b1[h], h=h, cc=cc, v_=v_: e.matmul(out=bb.ap[:, 0:257], lhsT=kt[h].ap, rhs=v_.ap[:, cc, h * 257:(h + 1) * 257], start=True, stop=True),
                         reads=[kt[h].b, v_.b], writes=[b1[h].b])
                for h in range(4):
                    ps_n = b2[h]
                    S.op("scalar", lambda e, ps_n=ps_n, h=h: e.activation(out=dd[h].ap, in_=ps_n.ap[:, 256:384], func=AF.Abs), reads=[ps_n.b], writes=[dd[h].b])
                    S.op(V, lambda e, h=h, cols=cols: e.tensor_tensor(out=dd[h].ap, in0=dd[h].ap, in1=EMt[h].ap[:, cols], op=ALU.max),
                         reads=[dd[h].b, EMt[h].b], writes=[dd[h].b])
                    S.op(V, lambda e, h=h: e.reciprocal(out=dd[h].ap, in_=dd[h].ap), reads=[dd[h].b], writes=[dd[h].b])
                    for e_ in range(2):
                        S.op(V, lambda e, ps_n=ps_n, h=h, cols=cols, e_=e_: e.tensor_tensor(out=hT[h].ap[:, e_, cols], in0=ps_n.ap[:, e_ * 128:(e_ + 1) * 128], in1=dd[h].ap, op=ALU.mult),
                             reads=[ps_n.b, dd[h].b], writes=[hT[h].b])
                    S.op(V, lambda e, bb=b1[h], h=h: e.scalar_tensor_tensor(out=C32[h].ap, in0=C32[h].ap, scalar=wk[h].ap[:, 1:2], in1=bb.ap[:, 0:257], op0=ALU.mult, op1=ALU.add),
                         reads=[C32[h].b, b1[h].b], sreads=[wk[h].b], writes=[C32[h].b])
                    S.op("scalar", lambda e, h=h: e.copy(out=Cbf[h].ap, in_=C32[h].ap[:, 0:256]), reads=[C32[h].b], writes=[Cbf[h].b])
                    S.op("gpsimd", lambda e, h=h: e.tensor_copy(out=nrep[h].ap, in_=C32[h].ap[:, 256:257].to_broadcast([128, 128])), reads=[C32[h].b], writes=[nrep[h].b])
            for h in range(4):
                S.op("scalar", lambda e, h=h: e.activation(out=sq2.ap, in_=hT[h].ap, func=AF.Square), reads=[hT[h].b], writes=[sq2.b])
                pm = self.psum()
                for e_ in range(2):
                    S.op("tensor", lambda e, pm=pm, e_=e_: e.matmul(out=pm.ap, lhsT=self.onesb.ap, rhs=sq2.ap[:, e_, :], start=(e_ == 0), stop=(e_ == 1)),
                         reads=[self.onesb.b, sq2.b], writes=[pm.b])
                S.op("scalar", lambda e, pm=pm: e.activation(out=rs.ap, in_=pm.ap, func=AF.Sqrt, bias=self.epsb.ap[:, 0:1], scale=1.0 / 256), reads=[pm.b, self.epsb.b], writes=[rs.b])
                S.op(V, lambda e: e.reciprocal(out=rs.ap, in_=rs.ap), reads=[rs.b], writes=[rs.b])
                os_ = ostg[oc % 2]
                oc += 1
                for e_ in range(2):
                    S.op(V, lambda e, h=h, e_=e_: e.scalar_tensor_tensor(out=tmpf.ap[:, e_, :], in0=hT[h].ap[:, e_, :], scalar=hn.ap[:, 2 * h + e_:2 * h + e_ + 1], in1=rs.ap,
                                                                       op0=ALU.mult, op1=ALU.mult), reads=[hT[h].b, rs.b, hn.b], writes=[tmpf.b])
                S.op("gpsimd", lambda e, h=h, os_=os_: e.tensor_tensor(out=os_.ap, in0=tmpf.ap, in1=ogt[h].ap, op=ALU.mult), reads=[tmpf.b, ogt[h].b], writes=[os_.b])
                S.dma("sync", lambda e, h=h, os_=os_, sl=sl: e.dma_start(out=self.MIX.ap[h * 256:(h + 1) * 256, sl].rearrange("(e p) t -> p e t", p=128), in_=os_.ap),
                      reads=[os_.b], writes=[self.MIX.b], accum=True)
        S.barrier()

    def phase_dump(self):
        S = self.S
        ar = self.ar
        ar.reset()
        xt = ar.alloc([128, 8, TS], F32, "dxt")
        otm = [ar.alloc([128, D], F32, f"dotm{k}") for k in range(2)]
        for i in range(NT):
            S.dma("sync", lambda e, i=i: e.dma_start(out=xt.ap, in_=self.xt_tile_ap(i)), reads=[self.XT.b], writes=[xt.b])
            ev = 0
            for s_ in range(4):
                o_ = otm[s_ % 2]
                for half in range(2):
                    p = self.psum()
                    for cc in range(4):
                        c = half * 4 + cc
                        S.op("tensor", lambda e, p=p, cc=cc, c=c, s_=s_: e.transpose(out=p.ap[:, cc * 128:(cc + 1) * 128], in_=xt.ap[:, c, s_ * 128:(s_ + 1) * 128], identity=self.ident.ap),
                             reads=[xt.b, self.ident.b], writes=[p.b])
                    self.evac(ev, o_.ap[:, half * 512:(half + 1) * 512], p.ap, [p.b], [o_.b]); ev += 1
                S.dma("sync", lambda e, o_=o_, i=i, s_=s_: e.dma_start(out=self.out[i * TS + s_ * 128:i * TS + (s_ + 1) * 128, :], in_=o_.ap), reads=[o_.b], writes=[self.b_out], accum=True)
        S.barrier()

    def build(self):
        sa = self.stop_after
        self.phase_l0_inproj()
        if sa == "inproj0":
            self.S.emit(); return self.nc
        self.phase_fox()
        self.phase_diff()
        self.phase_outproj_xattn(0, self.attn_w_out)
        if sa in ("mix0", "xat0"):
            self.phase_dump(); self.S.emit(); return self.nc
        self.phase_ffn(0)
        if sa == "ffn0":
            self.phase_dump(); self.S.emit(); return self.nc
        self.phase_l1_inproj()
        self.phase_mlstm()
        self.phase_outproj_xattn(1, self.mlstm_w_out)
        if sa in ("mix1", "xat1"):
            self.phase_dump(); self.S.emit(); return self.nc
        self.phase_ffn(1, final=True)
        self.S.emit()
        return self.nc


def make_consts():
    k = np.arange(128)[:, None]
    q = np.arange(128)[None, :]
    c = {}
    c["c_ident"] = np.eye(128, dtype=np.float32)
    c["c_trineg"] = np.where(q >= k, 0.0, NEG).astype(np.float32)
    c["c_dneg"] = np.where((k >= 64) & (q < 64), NEG, 0.0).astype(np.float32)
    c["c_tri01"] = (q >= k).astype(np.float32)
    sel = np.zeros((4, 4, 128), np.float32)
    for h in range(4):
        sel[h, h, :] = 1.0
    c["c_sel"] = sel
    return c


def make_in_maps(inputs, ncores=8):
    f = lambda a: np.ascontiguousarray(np.asarray(a, dtype=np.float32))
    shared = dict(
        gains=f(np.concatenate([inputs["mix_norm"], inputs["xattn_norm"], inputs["mem_norm"], inputs["ffn_norm"], np.asarray(inputs["final_norm"])[None, :]],
                               axis=0).reshape(9, 8, 128).transpose(2, 0, 1)),
        memgain=f(inputs["mem_norm"]),
        attn_w_in=f(inputs["attn_w_in"][0]), attn_w_out=f(inputs["attn_w_out"][0]),
        mlstm_w_in=f(inputs["mlstm_w_in"][0]), mlstm_w_out=f(inputs["mlstm_w_out"][0]),
        xattn_wq=f(inputs["xattn_wq"]), xattn_wkv=f(inputs["xattn_wkv"]), xattn_wo=f(inputs["xattn_wo"]),
        ffn_w_up=f(inputs["ffn_w_up"]), ffn_w_down=f(inputs["ffn_w_down"]),
        ffn_conv_w=f(np.asarray(inputs["ffn_conv_w"]).reshape(2, 3, 44, 128).transpose(0, 3, 1, 2)),
        ffn_conv_b=f(np.asarray(inputs["ffn_conv_b"]).reshape(2, 44, 128).transpose(0, 2, 1)),
        fox_bf=f(np.asarray(inputs["attn_fox_bf"]).reshape(8, 1)),
        lam_in=f(np.stack([inputs["diff_lq1"][0], inputs["diff_lk1"][0], inputs["diff_lq2"][0], inputs["diff_lk2"][0]], axis=0)),
        subln=f(np.asarray(inputs["diff_subln"]).reshape(128, 1)),
        mconv=f(np.asarray(inputs["mlstm_conv_qk"][0]).reshape(4, 8, 128).transpose(2, 0, 1)),
        mgate_b=f(np.stack([inputs["mlstm_b_i"][0], inputs["mlstm_b_f"][0]], axis=1)),
        mhn=f(np.asarray(inputs["mlstm_head_norm"][0]).reshape(8, 128).T),
    )
    shared.update(make_consts())
    maps = []
    for c in range(ncores):
        b = c % 4
        m = dict(shared)
        m["x"] = f(inputs["x"][b])
        m["mem"] = f(inputs["mem"][b])
        maps.append(m)
    return maps


_CACHE = {}


def kernel(**inputs):
    if "nc" not in _CACHE:
        _CACHE["nc"] = Builder().build()
    nc = _CACHE["nc"]
    maps = make_in_maps(inputs, 8)
    res = run_bass_kernel_spmd(nc, maps, core_ids=list(range(8)))
    out = np.stack([np.asarray(res.results[b]["out"]) for b in range(4)], axis=0)
    return out.astype(np.float32)
```

```python
import math
import numpy as np
import ml_dtypes
import concourse.bass as bass
import concourse.mybir as mybir
from concourse.bass_utils import run_bass_kernel_spmd

F32 = mybir.dt.float32
BF16 = mybir.dt.bfloat16
AF = mybir.ActivationFunctionType
ALU = mybir.AluOpType

SAME_ENGINE_SYNC = True
EPOCH = 30000
NDMA_SEMS = 12

T = 4096
D = 1024
NT = 8
TS = 512
NB = 32
DFF = 2816
NH = 22
EPS = 1e-6
NEG = -30000.0


class Buf:
    def __init__(self, name):
        self.name = name
        self.writers = []
        self.readers = []


def _compress(toks):
    mx = {}
    for k, v in toks:
        if mx.get(k, 0) < v:
            mx[k] = v
    return list(mx.items())


class Sched:
    ENGS = ("tensor", "scalar", "vector", "gpsimd", "sync")

    def __init__(self, nc):
        self.nc = nc
        self.ops = {e: [] for e in self.ENGS}
        self.sems = {}
        self.eng_epoch = {e: 0 for e in self.ENGS}
        self.eng_cnt = {e: 0 for e in self.ENGS}
        self.known = {e: {} for e in self.ENGS}
        self.dma_rr = {e: 0 for e in self.ENGS}
        self.dma_cnt = {}
        self.all_tokens = {}
        self.n_ops = 0

    def _sem(self, key):
        if key not in self.sems:
            self.sems[key] = self.nc.alloc_semaphore(name="s_" + "_".join(str(k) for k in key))
        return self.sems[key]

    def _collect(self, eng, reads, writes, sreads=(), chain=False, accum=False):
        need = {}
        force = {}
        for b_ in sreads:
            for k, v in b_.writers:
                if force.get(k, 0) < v:
                    force[k] = v

        def add(tok):
            k, v = tok
            if need.get(k, 0) < v:
                need[k] = v
        for b in reads:
            for t in b.writers:
                add(t)
        for b in sreads:
            for t in b.writers:
                add(t)
        for b in writes:
            for t in b.writers:
                if chain and t[0][0] == "e" and t[0][1] == eng:
                    continue
                if accum and t[0][0] == "d":
                    continue
                add(t)
            for t in b.readers:
                add(t)
        waits = []
        for k, v in need.items():
            if k[0] == "e" and k[1] == eng and not SAME_ENGINE_SYNC and force.get(k, 0) < v:
                continue
            if self.known[eng].get(k, 0) >= v:
                continue
            self.known[eng][k] = v
            waits.append((k, v))
        return waits

    def _update(self, tok, reads, writes, accum=False):
        for b in reads:
            b.readers.append(tok)
            if len(b.readers) > 48:
                b.readers = _compress(b.readers)
        for b in writes:
            if accum:
                b.writers.append(tok)
                if len(b.writers) > 48:
                    b.writers = _compress(b.writers)
            else:
                b.writers = [tok]
                b.readers = []
        self.all_tokens[tok[0]] = tok[1]

    def op(self, eng, fn, reads=(), writes=(), sreads=(), chain=False):
        if eng == "tensor":
            chain = True
        waits = self._collect(eng, reads, writes, sreads, chain)
        reads = list(reads) + list(sreads)
        if self.eng_cnt[eng] >= EPOCH:
            self.eng_epoch[eng] += 1
            self.eng_cnt[eng] = 0
        self.eng_cnt[eng] += 1
        key = ("e", eng, self.eng_epoch[eng])
        tok = (key, self.eng_cnt[eng])
        self._sem(key)
        self.ops[eng].append((waits, fn, (key, 1)))
        self._update(tok, reads, writes)
        self.n_ops += 1
        return tok

    def dma(self, eng, fn, reads=(), writes=(), accum=False):
        waits = self._collect(eng, reads, writes, accum=accum)
        i = self.dma_rr[eng]
        self.dma_rr[eng] = (i + 1) % NDMA_SEMS
        key = ("d", eng, i)
        self._sem(key)
        prev = self.dma_cnt.get(key, 0)
        if prev > 0 and self.known[eng].get(key, 0) < prev:
            self.known[eng][key] = prev
            waits.append((key, prev))
        val = prev + 16
        self.dma_cnt[key] = val
        tok = (key, val)
        self.ops[eng].append((waits, fn, (key, 16)))
        self._update(tok, reads, writes, accum=accum)
        self.n_ops += 1
        return tok

    def barrier(self):
        for eng in self.ENGS:
            waits = []
            for k, v in self.all_tokens.items():
                if k[0] == "e" and k[1] == eng:
                    continue
                if self.known[eng].get(k, 0) >= v:
                    continue
                self.known[eng][k] = v
                waits.append((k, v))
            if waits:
                self.ops[eng].append((waits, None, None))

    def emit(self):
        nc = self.nc
        self.barrier()
        sems = self.sems
        with nc.Block() as block:
            def body(eng_name):
                def f(e):
                    for waits, fn, inc in self.ops[eng_name]:
                        for k, v in waits:
                            e.wait_ge(sems[k], v)
                        if fn is not None:
                            ins = fn(e)
                            ins.then_inc(sems[inc[0]], inc[1])
                return f
            block.sync(body("sync"))
            block.tensor(body("tensor"))
            block.scalar(body("scalar"))
            block.vector(body("vector"))
            block.gpsimd(body("gpsimd"))


class TT:
    def __init__(self, ap, name):
        self.ap = ap
        self.b = Buf(name)

    def __getitem__(self, k):
        return self.ap[k]


class Arena:
    def __init__(self, raw_ap, nwords):
        self.raw = raw_ap
        self.n = nwords
        self.off = 0
        self.cnt = 0

    def reset(self):
        self.off = 0

    def alloc(self, shape, dtype, name=None):
        free = 1
        for s in shape[1:]:
            free *= s
        words = free if dtype == F32 else (free + 1) // 2
        words = (words + 7) // 8 * 8
        assert self.off + words <= self.n, f"arena overflow {name} {self.off}+{words}>{self.n}"
        v = self.raw[0:shape[0], self.off:self.off + words]
        self.off += words
        if dtype != F32:
            v = v.bitcast(BF16)
        v = v[:, 0:free]
        if len(shape) == 3:
            v = v.rearrange("p (a b) -> p a b", a=shape[1])
        elif len(shape) == 4:
            v = v.rearrange("p (a b c) -> p a b c", a=shape[1], b=shape[2])
        self.cnt += 1
        return TT(v, name or f"ar{self.cnt}")


class Builder:
    def __init__(self, stop_after=None, debug=False):
        self.stop_after = stop_after
        self.debug = debug
        nc = bass.Bass("TRN2", target_bir_lowering=False)
        self.nc = nc
        self.S = Sched(nc)
        self.rr = 0
        self._declare_io()
        self._alloc()

    def _din(self, name, shape, dtype=F32):
        return self.nc.dram_tensor(name, list(shape), dtype, kind="ExternalInput").ap()

    def _dscr(self, name, shape, dtype):
        if self.debug:
            return TT(self.nc.dram_tensor(name, list(shape), dtype, kind="ExternalOutput").ap(), name)
        return TT(self.nc.dram_tensor(name, list(shape), dtype).ap(), name)

    def _declare_io(self):
        d = self._din
        self.x_in = d("x", [T, D])
        self.mem_in = d("mem", [256, D])
        self.gains = d("gains", [128, 9, 8])
        self.memgain = d("memgain", [2, D])
        self.attn_w_in = d("attn_w_in", [D, 3080])
        self.attn_w_out = d("attn_w_out", [D, D])
        self.mlstm_w_in = d("mlstm_w_in", [D, 3080])
        self.mlstm_w_out = d("mlstm_w_out", [D, D])
        self.xattn_wq = d("xattn_wq", [2, D, D])
        self.xattn_wkv = d("xattn_wkv", [2, D, 2 * D])
        self.xattn_wo = d("xattn_wo", [2, D, D])
        self.ffn_w_up = d("ffn_w_up", [2, D, 2 * DFF])
        self.ffn_w_down = d("ffn_w_down", [2, DFF, D])
        self.ffn_conv_w = d("ffn_conv_w", [2, 128, 3, 44])
        self.ffn_conv_b = d("ffn_conv_b", [2, 128, 44])
        self.fox_bf = d("fox_bf", [8, 1])
        self.lam_in = d("lam_in", [4, 64])
        self.subln = d("subln", [128, 1])
        self.mconv = d("mconv", [128, 4, 8])
        self.mgate_b = d("mgate_b", [4, 2])
        self.mhn = d("mhn", [128, 8])
        self.c_ident = d("c_ident", [128, 128])
        self.c_trineg = d("c_trineg", [128, 128])
        self.c_dneg = d("c_dneg", [128, 128])
        self.c_tri01 = d("c_tri01", [128, 128])
        self.c_sel = d("c_sel", [4, 4, 128])
        self.out = self.nc.dram_tensor("out", [T, D], F32, kind="ExternalOutput").ap()
        self.b_out = Buf("out")
        s = self._dscr
        self.XT = s("XT", [D, T], F32)
        self.QA = s("QA", [8, 70, T], BF16)
        self.KA = s("KA", [8, 70, T], BF16)
        self.VA = s("VA", [T, 768], BF16)
        self.DQ = s("DQ", [4, 128, T], BF16)
        self.DK = s("DK", [4, 128, T], BF16)
        self.DV = s("DV", [T, 512], BF16)
        self.MIX = s("MIX", [D, T], BF16)
        self.QK = s("QK", [8, 128, T], BF16)
        self.VM = s("VM", [T, 1028], BF16)
        self.OG = s("OG", [D, T], BF16)
        self.DBG = s("DBG", [8, 128, TS], F32)

    def _alloc(self):
        nc = self.nc
        S = self.S

        def sb(name, shape, dtype):
            return TT(nc.alloc_sbuf_tensor(name, list(shape), dtype)[:], name)
        self.ident = sb("ident", [128, 128], F32)
        self.identb = sb("identb", [128, 128], BF16)
        self.onesb = sb("onesb", [128, 128], BF16)
        self.trineg = sb("trineg", [128, 128], F32)
        self.dneg = sb("dneg", [128, 128], F32)
        self.tri01 = sb("tri01", [128, 128], F32)
        self.sel = sb("sel", [4, 4, 128], F32)
        self.gain = sb("gain", [128, 9, 8], F32)
        self.epsb = sb("epsb", [128, 1], F32)
        self.small = sb("small", [128, 64], F32)
        self.ps = [TT(nc.alloc_psum_tensor(f"ps{i}", [128, 512], F32)[:], f"ps{i}") for i in range(8)]
        nwords = (nc.sbuf_bytes_remaining - 2048) // 4
        raw = nc.alloc_sbuf_tensor("arena", [128, nwords], F32)
        self.ar = Arena(raw[:], nwords)
        ld = lambda dst, src: S.dma("sync", lambda e: e.dma_start(out=dst.ap, in_=src), writes=[dst.b], accum=True)
        ld(self.ident, self.c_ident)
        ld(self.trineg, self.c_trineg)
        ld(self.dneg, self.c_dneg)
        ld(self.tri01, self.c_tri01)
        ld(self.sel, self.c_sel)
        S.dma("sync", lambda e: e.dma_start(out=self.gain.ap, in_=self.gains),
              writes=[self.gain.b])
        S.op("vector", lambda e: e.tensor_copy(out=self.identb.ap, in_=self.ident.ap), reads=[self.ident.b], writes=[self.identb.b])
        S.op("vector", lambda e: e.memset(self.onesb.ap, 1.0), writes=[self.onesb.b])
        S.op("vector", lambda e: e.memset(self.epsb.ap, EPS), writes=[self.epsb.b])

    def psum(self):
        p = self.ps[self.rr % 8]
        self.rr += 1
        return p

    def load_w(self, dst, src_ap, kc_list=None):
        S = self.S
        KC = dst.ap.shape[1]
        v = src_ap.rearrange("(kc p) n -> p kc n", p=128)
        for kc in range(KC):
            S.dma("gpsimd", lambda e, kc=kc: e.dma_start(out=dst.ap[:, kc, :], in_=v[:, kc, :]), writes=[dst.b], accum=True)

    def evac(self, i, out_ap, in_ap, reads, writes, scale=None):
        S = self.S
        if i % 2 == 0:
            if scale is None:
                S.op("scalar", lambda e: e.copy(out=out_ap, in_=in_ap), reads=reads, writes=writes)
            else:
                S.op("scalar", lambda e: e.mul(out=out_ap, in_=in_ap, mul=scale), reads=reads, writes=writes)
        else:
            if scale is None:
                S.op("vector", lambda e: e.tensor_copy(out=out_ap, in_=in_ap), reads=reads, writes=writes)
            else:
                S.op("vector", lambda e: e.tensor_scalar(out=out_ap, in0=in_ap, scalar1=scale, scalar2=None, op0=ALU.mult),
                     reads=reads, writes=writes)

    def rmsnorm(self, xt, xn, sq, rs, gidx, nfeat=D, p=None):
        S = self.S
        S.op("scalar", lambda e: e.activation(out=sq.ap, in_=xt.ap, func=AF.Square), reads=[xt.b], writes=[sq.b])
        if p is None:
            p = self.psum()
        N = xt.ap.shape[2]
        for c in range(8):
            S.op("tensor", lambda e, c=c: e.matmul(out=p.ap[:, 0:N], lhsT=self.onesb.ap, rhs=sq.ap[:, c, :], start=(c == 0), stop=(c == 7)),
                 reads=[self.onesb.b, sq.b], writes=[p.b])
        S.op("scalar", lambda e: e.activation(out=rs.ap, in_=p.ap[:, 0:N], func=AF.Sqrt, bias=self.epsb.ap[:, 0:1], scale=1.0 / nfeat),
             reads=[p.b, self.epsb.b], writes=[rs.b])
        S.op("vector", lambda e: e.reciprocal(out=rs.ap, in_=rs.ap), reads=[rs.b], writes=[rs.b])
        for c in range(8):
            eng = "vector"
            S.op(eng, lambda e, c=c: e.scalar_tensor_tensor(out=xn.ap[:, c, :], in0=xt.ap[:, c, :], scalar=self.gain.ap[:, gidx, c:c + 1],
                                                              in1=rs.ap, op0=ALU.mult, op1=ALU.mult),
                 reads=[xt.b, rs.b, self.gain.b], writes=[xn.b])

    def proj_fm(self, p, w, col0, M, xn, N=TS, prow=0):
        S = self.S
        KC = w.ap.shape[1]
        for kc in range(KC):
            S.op("tensor", lambda e, kc=kc: e.matmul(out=p.ap[prow:prow + M, 0:N], lhsT=w.ap[:, kc, col0:col0 + M], rhs=xn.ap[:, kc, 0:N],
                                                     start=(kc == 0), stop=(kc == KC - 1)),
                 reads=[w.b, xn.b], writes=[p.b])

    def proj_tm(self, p, xn, tok0, w, col0, ncol):
        S = self.S
        KC = w.ap.shape[1]
        for kc in range(KC):
            S.op("tensor", lambda e, kc=kc: e.matmul(out=p.ap[:, 0:ncol], lhsT=xn.ap[:, kc, tok0:tok0 + 128], rhs=w.ap[:, kc, col0:col0 + ncol],
                                                     start=(kc == 0), stop=(kc == KC - 1)),
                 reads=[w.b, xn.b], writes=[p.b])

    def xt_tile_ap(self, i):
        return self.XT.ap.rearrange("(c p) t -> p c t", p=128)[:, :, i * TS:(i + 1) * TS]

    def phase_l0_inproj(self):
        S = self.S
        ar = self.ar
        ar.reset()
        lfn = ar.alloc([8, T], F32, "lfn")
        self.lfn = lfn
        nbf = ar.alloc([8, 1], F32, "nbf")
        tmp8 = ar.alloc([8, TS], F32, "tmp8")
        mark = ar.off
        w = ar.alloc([128, 8, 3080], BF16, "w_in0")
        self.load_w(w, self.attn_w_in)
        xtm = [ar.alloc([128, 4, D], F32, f"xtm{k}") for k in range(2)]
        xt = [ar.alloc([128, 8, TS], F32, f"xt{k}") for k in range(2)]
        xn = [ar.alloc([128, 8, TS], BF16, f"xn{k}") for k in range(2)]
        sq = ar.alloc([128, 8, TS], BF16, "sq")
        rs = ar.alloc([128, TS], F32, "rs")
        stg = [ar.alloc([128, TS], BF16, f"stg{k}") for k in range(6)]
        vst = [ar.alloc([128, 4, 768], BF16, f"vst{k}") for k in range(2)]
        dvst = [ar.alloc([128, 4, 512], BF16, f"dvst{k}") for k in range(2)]
        for k in range(2):
            S.op("gpsimd", lambda e, k=k: e.memset(vst[k].ap, 1.0), writes=[vst[k].b])
        S.dma("sync", lambda e: e.dma_start(out=nbf.ap, in_=self.fox_bf), writes=[nbf.b])
        S.op("vector", lambda e: e.tensor_scalar(out=nbf.ap, in0=nbf.ap, scalar1=-1.0, scalar2=None, op0=ALU.mult), reads=[nbf.b], writes=[nbf.b])
        si = 0
        ev = 0
        evb = [0]

        def prep(i):
            xm = xtm[i % 2]
            x_ = xt[i % 2]
            xn_ = xn[i % 2]
            S.dma("sync", lambda e: e.dma_start(out=xm.ap, in_=self.x_in[i * TS:(i + 1) * TS, :].rearrange("(s p) d -> p s d", p=128)), writes=[xm.b])
            for c in range(8):
                p = self.psum()
                for s_ in range(4):
                    S.op("tensor", lambda e, s_=s_, c=c, p=p: e.transpose(out=p.ap[:, s_ * 128:(s_ + 1) * 128], in_=xm.ap[:, s_, c * 128:(c + 1) * 128], identity=self.ident.ap),
                         reads=[xm.b, self.ident.b], writes=[p.b])
                self.evac(evb[0], x_.ap[:, c, :], p.ap, [p.b], [x_.b]); evb[0] += 1
            S.dma("sync", lambda e: e.dma_start(out=self.xt_tile_ap(i), in_=x_.ap), reads=[x_.b], writes=[self.XT.b], accum=True)
            self.rmsnorm(x_, xn_, sq, rs, 0)

        prep(0)
        for i in range(NT):
            x_ = xt[i % 2]
            xn_ = xn[i % 2]
            if i + 1 < NT:
                prep(i + 1)
            for grp, (col0, dst, scale) in enumerate([(0, self.QA, 0.125), (512, self.KA, None), (1544, self.DQ, 0.125), (2056, self.DK, None)]):
                for j in range(4):
                    p = self.psum()
                    self.proj_fm(p, w, col0 + j * 128, 128, xn_)
                    st = stg[si % 6]; si += 1
                    self.evac(ev, st.ap, p.ap, [p.b], [st.b], scale=scale); ev += 1
                    if grp < 2:
                        for hh in range(2):
                            S.dma("sync", lambda e, st=st, dst=dst, j=j, hh=hh, i=i: e.dma_start(
                                out=dst.ap[2 * j + hh, 0:64, i * TS:(i + 1) * TS], in_=st.ap[hh * 64:(hh + 1) * 64, :]),
                                reads=[st.b], writes=[dst.b], accum=True)
                    else:
                        S.dma("sync", lambda e, st=st, dst=dst, j=j, i=i: e.dma_start(out=dst.ap[j, :, i * TS:(i + 1) * TS], in_=st.ap),
                              reads=[st.b], writes=[dst.b], accum=True)
            vs = vst[i % 2]
            ds = dvst[i % 2]
            for s_ in range(4):
                p = self.psum()
                self.proj_tm(p, xn_, s_ * 128, w, 1024, 512)
                pv = p.ap.rearrange("p (g e c) -> p g e c", g=4, e=2)
                ov = vs.ap[:, s_, :].rearrange("p (g c) -> p g c", c=192)
                S.op("scalar", lambda e, ov=ov, pv=pv: e.copy(out=ov[:, :, 0:64], in_=pv[:, :, 0, :]), reads=[p.b], writes=[vs.b])
                S.op("vector", lambda e, ov=ov, pv=pv: e.tensor_copy(out=ov[:, :, 128:192], in_=pv[:, :, 1, :]), reads=[p.b], writes=[vs.b])
                p2 = self.psum()
                self.proj_tm(p2, xn_, s_ * 128, w, 2568, 512)
                self.evac(ev, ds.ap[:, s_, :], p2.ap, [p2.b], [ds.b]); ev += 1
            S.dma("sync", lambda e, vs=vs, i=i: e.dma_start(out=self.VA.ap[i * TS:(i + 1) * TS, :].rearrange("(s p) c -> p s c", p=128), in_=vs.ap),
                  reads=[vs.b], writes=[self.VA.b], accum=True)
            S.dma("sync", lambda e, ds=ds, i=i: e.dma_start(out=self.DV.ap[i * TS:(i + 1) * TS, :].rearrange("(s p) c -> p s c", p=128), in_=ds.ap),
                  reads=[ds.b], writes=[self.DV.b], accum=True)
            p = self.psum()
            self.proj_fm(p, w, 1536, 8, xn_)
            S.op("scalar", lambda e, p=p: e.activation(out=tmp8.ap, in_=p.ap[0:8, :], func=AF.Exp, bias=nbf.ap[:, 0:1], scale=-1.0),
                 reads=[p.b, nbf.b], writes=[tmp8.b])
            S.op("scalar", lambda e, i=i: e.activation(out=lfn.ap[:, i * TS:(i + 1) * TS], in_=tmp8.ap, func=AF.Ln, bias=1.0, scale=1.0),
                 reads=[tmp8.b], writes=[lfn.b])
        S.barrier()
        ar.off = mark
        cs = ar.alloc([8, T], F32, "cs")
        ones8 = ar.alloc([8, T], F32, "ones8")
        r32 = ar.alloc([8, T], F32, "r32")
        t32 = ar.alloc([8, T], F32, "t32")
        parts = [ar.alloc([8, T], BF16, f"part{k}") for k in range(6)]
        onesb8 = ar.alloc([8, T], BF16, "onesb8")
        S.op("gpsimd", lambda e: e.memset(ones8.ap, 1.0), writes=[ones8.b])
        S.op("gpsimd", lambda e: e.memset(onesb8.ap, 1.0), writes=[onesb8.b])
        for i in range(NT):
            init = 0.0 if i == 0 else cs.ap[:, i * TS - 1:i * TS]
            S.op("vector", lambda e, i=i, init=init: e.tensor_tensor_scan(out=cs.ap[:, i * TS:(i + 1) * TS], data0=ones8.ap[:, i * TS:(i + 1) * TS],
                                                                    data1=lfn.ap[:, i * TS:(i + 1) * TS], initial=init, op0=ALU.mult, op1=ALU.add),
                 reads=[ones8.b, lfn.b], sreads=[cs.b], writes=[cs.b])
        V = "vector"
        S.op(V, lambda e: e.tensor_copy(out=parts[0].ap, in_=cs.ap), reads=[cs.b], writes=[parts[0].b])
        S.op(V, lambda e: e.tensor_copy(out=t32.ap, in_=parts[0].ap), reads=[parts[0].b], writes=[t32.b])
        S.op(V, lambda e: e.tensor_tensor(out=r32.ap, in0=cs.ap, in1=t32.ap, op=ALU.subtract), reads=[cs.b, t32.b], writes=[r32.b])
        S.op(V, lambda e: e.tensor_copy(out=parts[1].ap, in_=r32.ap), reads=[r32.b], writes=[parts[1].b])
        S.op(V, lambda e: e.tensor_copy(out=t32.ap, in_=parts[1].ap), reads=[parts[1].b], writes=[t32.b])
        S.op(V, lambda e: e.tensor_tensor(out=r32.ap, in0=r32.ap, in1=t32.ap, op=ALU.subtract), reads=[r32.b, t32.b], writes=[r32.b])
        S.op(V, lambda e: e.tensor_copy(out=parts[2].ap, in_=r32.ap), reads=[r32.b], writes=[parts[2].b])
        for k in range(3):
            S.op(V, lambda e, k=k: e.tensor_scalar(out=parts[3 + k].ap, in0=parts[k].ap, scalar1=-1.0, scalar2=None, op0=ALU.mult),
                 reads=[parts[k].b], writes=[parts[3 + k].b])
        for r in range(3):
            S.dma("sync", lambda e, r=r: e.dma_start(out=self.QA.ap[:, 64 + r, :], in_=parts[3 + r].ap), reads=[parts[3 + r].b], writes=[self.QA.b], accum=True)
            S.dma("sync", lambda e, r=r: e.dma_start(out=self.QA.ap[:, 67 + r, :], in_=onesb8.ap), reads=[onesb8.b], writes=[self.QA.b], accum=True)
            S.dma("sync", lambda e, r=r: e.dma_start(out=self.KA.ap[:, 64 + r, :], in_=onesb8.ap), reads=[onesb8.b], writes=[self.KA.b], accum=True)
            S.dma("sync", lambda e, r=r: e.dma_start(out=self.KA.ap[:, 67 + r, :], in_=parts[r].ap), reads=[parts[r].b], writes=[self.KA.b], accum=True)
        S.barrier()

    def phase_fox(self):
        S = self.S
        ar = self.ar
        ar.reset()
        va = ar.alloc([128, NB, 768], BF16, "va")
        S.dma("sync", lambda e: e.dma_start(out=va.ap, in_=self.VA.ap.rearrange("(b p) c -> p b c", p=128)), reads=[self.VA.b], writes=[va.b])
        qa = [ar.alloc([70, T], BF16, f"qa{k}") for k in range(2)]
        ka = [ar.alloc([70, T], BF16, f"ka{k}") for k in range(2)]
        pt = [ar.alloc([128, TS], BF16, f"pt{k}") for k in range(6)]
        rd = [ar.alloc([128, TS], F32, f"rd{k}") for k in range(2)]
        rd2 = [ar.alloc([128, TS], F32, f"rd2{k}") for k in range(2)]
        ost = [ar.alloc([128, TS], BF16, f"ost{k}") for k in range(2)]
        acc = [self.ps[6], self.ps[7]]
        sps = self.ps[0:6]
        A, B = [], []
        u = 0
        tcount = 0
        for h in range(8):
            q_ = qa[h % 2]
            k_ = ka[h % 2]
            pair = h // 2
            odd = h % 2
            vc0 = pair * 192 + (64 if odd else 0)
            nrow = slice(64, 128) if odd else slice(0, 64)
            drow = slice(0, 64) if odd else slice(64, 128)
            for i in range(NT):
                a = acc[tcount % 2]
                rd_ = rd[tcount % 2]
                rd2_ = rd2[tcount % 2]
                os_ = ost[tcount % 2]
                tcount += 1
                nj = 4 * i + 4
                for j in range(nj):
                    r = j - 4 * i
                    c0 = 128 * r if r > 0 else 0
                    sp = sps[u % 6]
                    p_ = pt[u % len(pt)]
                    u += 1

                    def fa(h=h, q_=q_, k_=k_, i=i, j=j, r=r, c0=c0, sp=sp, p_=p_):
                        if i == 0 and j == 0:
                            S.dma("sync", lambda e: e.dma_start(out=q_.ap, in_=self.QA.ap[h]), reads=[self.QA.b], writes=[q_.b])
                            S.dma("sync", lambda e: e.dma_start(out=k_.ap, in_=self.KA.ap[h]), reads=[self.KA.b], writes=[k_.b])
                        S.op("tensor", lambda e: e.matmul(out=sp.ap[:, c0:TS], lhsT=k_.ap[0:70, j * 128:(j + 1) * 128], rhs=q_.ap[0:70, i * TS + c0:(i + 1) * TS],
                                                          start=True, stop=True), reads=[k_.b, q_.b], writes=[sp.b])
                        if r >= 0:
                            S.op("vector", lambda e: e.tensor_tensor(out=sp.ap[:, c0:c0 + 128], in0=sp.ap[:, c0:c0 + 128], in1=self.trineg.ap, op=ALU.add),
                                 reads=[sp.b, self.trineg.b], writes=[sp.b])
                        S.op("scalar", lambda e: e.activation(out=p_.ap[:, c0:TS], in_=sp.ap[:, c0:TS], func=AF.Exp), reads=[sp.b], writes=[p_.b])

                    def fb(h=h, i=i, j=j, nj=nj, c0=c0, p_=p_, a=a, rd_=rd_, rd2_=rd2_, os_=os_, vc0=vc0, nrow=nrow, drow=drow):
                        S.op("tensor", lambda e: e.matmul(out=a.ap[:, c0:TS], lhsT=va.ap[:, j, vc0:vc0 + 128], rhs=p_.ap[:, c0:TS], start=(j == 0), stop=(j == nj - 1)),
                             reads=[va.b, p_.b], writes=[a.b])
                        if j == nj - 1:
                            S.op("vector", lambda e: e.reciprocal(out=rd_.ap[drow, :], in_=a.ap[drow, :]), reads=[a.b], writes=[rd_.b])
                            S.dma("sync", lambda e: e.dma_start(out=rd2_.ap[nrow, :], in_=rd_.ap[drow, :]), reads=[rd_.b], writes=[rd2_.b])
                            S.op("vector", lambda e: e.tensor_tensor(out=os_.ap[nrow, :], in0=a.ap[nrow, :], in1=rd2_.ap[nrow, :], op=ALU.mult),
                                 reads=[a.b, rd2_.b], writes=[os_.b])
                            S.dma("sync", lambda e: e.dma_start(out=self.MIX.ap[h * 64:(h + 1) * 64, i * TS:(i + 1) * TS], in_=os_.ap[nrow, :]),
                                  reads=[os_.b], writes=[self.MIX.b], accum=True)
                    A.append(fa)
                    B.append(fb)
        LA = 3
        for idx in range(len(A) + LA):
            if idx < len(A):
                A[idx]()
            if idx >= LA:
                B[idx - LA]()
        S.barrier()

    def phase_diff(self):
        S = self.S
        ar = self.ar
        ar.reset()
        dv = ar.alloc([128, NB, 512], BF16, "dv")
        S.dma("sync", lambda e: e.dma_start(out=dv.ap, in_=self.DV.ap.rearrange("(b p) c -> p b c", p=128)), reads=[self.DV.b], writes=[dv.b])
        dq = [ar.alloc([128, T], BF16, f"dq{k}") for k in range(2)]
        dk = [ar.alloc([128, T], BF16, f"dk{k}") for k in range(2)]
        pt = [ar.alloc([128, TS], BF16, f"dpt{k}") for k in range(6)]
        f1 = ar.alloc([128, TS], F32, "f1")
        f2 = ar.alloc([128, TS], F32, "f2")
        f3 = ar.alloc([128, TS], F32, "f3")
        sqb = ar.alloc([128, TS], BF16, "sqb")
        ost = [ar.alloc([128, TS], BF16, f"dost{k}") for k in range(2)]
        lam = ar.alloc([128, 4, 64], F32, "lam")
        lamt = ar.alloc([128, 2, 64], F32, "lamt")
        lams = ar.alloc([128, 8], F32, "lams")
        sg = ar.alloc([128, 1], F32, "sg")
        S.dma("sync", lambda e: e.dma_start(out=lam.ap, in_=self.lam_in.rearrange("(o a) d -> o a d", o=1).to_broadcast([128, 4, 64])), writes=[lam.b])
        S.op("vector", lambda e: e.tensor_tensor(out=lamt.ap[:, 0, :], in0=lam.ap[:, 0, :], in1=lam.ap[:, 1, :], op=ALU.mult), reads=[lam.b], writes=[lamt.b])
        S.op("vector", lambda e: e.tensor_tensor(out=lamt.ap[:, 1, :], in0=lam.ap[:, 2, :], in1=lam.ap[:, 3, :], op=ALU.mult), reads=[lam.b], writes=[lamt.b])
        S.op("vector", lambda e: e.reduce_sum(out=lams.ap[:, 0:2], in_=lamt.ap, axis=mybir.AxisListType.X), reads=[lamt.b], writes=[lams.b])
        S.op("scalar", lambda e: e.activation(out=lams.ap[:, 2:4], in_=lams.ap[:, 0:2], func=AF.Exp), reads=[lams.b], writes=[lams.b])
        S.op("vector", lambda e: e.tensor_tensor(out=lams.ap[:, 4:5], in0=lams.ap[:, 3:4], in1=lams.ap[:, 2:3], op=ALU.subtract), reads=[lams.b], writes=[lams.b])
        S.op("vector", lambda e: e.tensor_scalar(out=lams.ap[:, 5:6], in0=lams.ap[:, 4:5], scalar1=-0.2, scalar2=None, op0=ALU.add), reads=[lams.b], writes=[lams.b])
        neglam = lams.ap[:, 5:6]
        S.dma("sync", lambda e: e.dma_start(out=sg.ap, in_=self.subln), writes=[sg.b])
        S.op("vector", lambda e: e.tensor_scalar(out=sg.ap, in0=sg.ap, scalar1=0.8, scalar2=None, op0=ALU.mult), reads=[sg.b], writes=[sg.b])
        num = [self.ps[0], self.ps[1]]
        den = [self.ps[2], self.ps[3]]
        sps = self.ps[4:8]
        A, B = [], []
        u = 0
        tcount = 0
        V = "vector"
        for h in range(4):
            q_ = dq[h % 2]
            k_ = dk[h % 2]
            for i in range(NT):
                nj = 4 * i + 4
                os_ = ost[tcount % 2]
                tcount += 1
                for j in range(nj):
                    r = j - 4 * i
                    c0 = 128 * r if r > 0 else 0
                    for m in range(2):
                        sp = sps[u % 4]
                        p_ = pt[u % len(pt)]
                        u += 1
                        mr = slice(m * 64, (m + 1) * 64)

                        def fa(h=h, q_=q_, k_=k_, i=i, j=j, m=m, r=r, c0=c0, sp=sp, p_=p_, mr=mr):
                            if i == 0 and j == 0 and m == 0:
                                S.dma("sync", lambda e: e.dma_start(out=q_.ap, in_=self.DQ.ap[h]), reads=[self.DQ.b], writes=[q_.b])
                                S.dma("sync", lambda e: e.dma_start(out=k_.ap, in_=self.DK.ap[h]), reads=[self.DK.b], writes=[k_.b])
                            S.op("tensor", lambda e: e.matmul(out=sp.ap[:, c0:TS], lhsT=k_.ap[mr, j * 128:(j + 1) * 128], rhs=q_.ap[mr, i * TS + c0:(i + 1) * TS],
                                                              start=True, stop=True), reads=[k_.b, q_.b], writes=[sp.b])
                            if r >= 0:
                                S.op("vector", lambda e: e.tensor_tensor(out=sp.ap[:, c0:c0 + 128], in0=sp.ap[:, c0:c0 + 128], in1=self.dneg.ap, op=ALU.add),
                                     reads=[sp.b, self.dneg.b], writes=[sp.b])
                            S.op("scalar", lambda e: e.activation(out=p_.ap[:, c0:TS], in_=sp.ap[:, c0:TS], func=AF.Exp), reads=[sp.b], writes=[p_.b])

                        def fb(h=h, i=i, j=j, m=m, nj=nj, c0=c0, p_=p_, os_=os_):
                            S.op("tensor", lambda e: e.matmul(out=num[m].ap[:, c0:TS], lhsT=dv.ap[:, j, h * 128:(h + 1) * 128], rhs=p_.ap[:, c0:TS], start=(j == 0), stop=(j == nj - 1)),
                                 reads=[dv.b, p_.b], writes=[num[m].b])
                            S.op("tensor", lambda e: e.matmul(out=den[m].ap[:, c0:TS], lhsT=self.onesb.ap, rhs=p_.ap[:, c0:TS], start=(j == 0), stop=(j == nj - 1)),
                                 reads=[self.onesb.b, p_.b], writes=[den[m].b])
                            if j == nj - 1 and m == 1:
                                S.op(V, lambda e: e.reciprocal(out=f1.ap, in_=den[0].ap), reads=[den[0].b], writes=[f1.b])
                                S.op(V, lambda e: e.reciprocal(out=f2.ap, in_=den[1].ap), reads=[den[1].b], writes=[f2.b])
                                S.op(V, lambda e: e.tensor_tensor(out=f1.ap, in0=num[0].ap, in1=f1.ap, op=ALU.mult), reads=[num[0].b, f1.b], writes=[f1.b])
                                S.op(V, lambda e: e.tensor_tensor(out=f2.ap, in0=num[1].ap, in1=f2.ap, op=ALU.mult), reads=[num[1].b, f2.b], writes=[f2.b])
                                S.op(V, lambda e: e.scalar_tensor_tensor(out=f3.ap, in0=f2.ap, scalar=neglam, in1=f1.ap, op0=ALU.mult, op1=ALU.add),
                                     reads=[f1.b, f2.b], sreads=[lams.b], writes=[f3.b])
                                S.op("scalar", lambda e: e.activation(out=sqb.ap, in_=f3.ap, func=AF.Square), reads=[f3.b], writes=[sqb.b])
                                pm = self.ps[4 + (tcount_box[0] % 4)]
                                tcount_box[0] += 1
                                S.op("tensor", lambda e: e.matmul(out=pm.ap, lhsT=self.onesb.ap, rhs=sqb.ap, start=True, stop=True),
                                     reads=[self.onesb.b, sqb.b], writes=[pm.b])
                                S.op("scalar", lambda e: e.activation(out=f1.ap, in_=pm.ap, func=AF.Sqrt, bias=self.epsb.ap[:, 0:1], scale=1.0 / 128),
                                     reads=[pm.b, self.epsb.b], writes=[f1.b])
                                S.op(V, lambda e: e.reciprocal(out=f1.ap, in_=f1.ap), reads=[f1.b], writes=[f1.b])
                                S.op(V, lambda e: e.scalar_tensor_tensor(out=os_.ap, in0=f3.ap, scalar=sg.ap[:, 0:1], in1=f1.ap, op0=ALU.mult, op1=ALU.mult),
                                     reads=[f3.b, f1.b], sreads=[sg.b], writes=[os_.b])
                                S.dma("sync", lambda e: e.dma_start(out=self.MIX.ap[512 + h * 128:512 + (h + 1) * 128, i * TS:(i + 1) * TS], in_=os_.ap),
                                      reads=[os_.b], writes=[self.MIX.b], accum=True)
                        A.append(fa)
                        B.append(fb)
        tcount_box = [0]
        LA = 3
        for idx in range(len(A) + LA):
            if idx < len(A):
                A[idx]()
            if idx >= LA:
                B[idx - LA]()
        S.barrier()

    def phase_outproj_xattn(self, layer, w_out_dram):
        S = self.S
        ar = self.ar
        ar.reset()
        mkT = ar.alloc([128, 8, 256], BF16, "mkT")
        mv = ar.alloc([128, 2, D], BF16, "mv")
        mark = ar.off
        wkv = ar.alloc([128, 8, 2048], BF16, "wkv")
        self.load_w(wkv, self.xattn_wkv[layer])
        memt = ar.alloc([128, 2, D], F32, "memt")
        memn = ar.alloc([128, 2, D], BF16, "memn")
        mscr = ar.alloc([128, D], F32, "mscr")
        mss = ar.alloc([128, 4], F32, "mss")
        gmem = ar.alloc([128, D], F32, "gmem")
        memnT = ar.alloc([128, 8, 256], BF16, "memnT")
        S.dma("sync", lambda e: e.dma_start(out=memt.ap, in_=self.mem_in.rearrange("(s p) d -> p s d", p=128)), writes=[memt.b])
        S.dma("sync", lambda e: e.dma_start(out=gmem.ap, in_=self.memgain[layer:layer + 1, :].to_broadcast([128, D])), writes=[gmem.b])
        for s_ in range(2):
            S.op("vector", lambda e, s_=s_: e.tensor_tensor(out=mscr.ap, in0=memt.ap[:, s_, :], in1=memt.ap[:, s_, :], op=ALU.mult), reads=[memt.b], writes=[mscr.b])
            S.op("vector", lambda e, s_=s_: e.reduce_sum(out=mss.ap[:, s_:s_ + 1], in_=mscr.ap, axis=mybir.AxisListType.X), reads=[mscr.b], writes=[mss.b])
        S.op("scalar", lambda e: e.activation(out=mss.ap[:, 2:4], in_=mss.ap[:, 0:2], func=AF.Sqrt, bias=self.epsb.ap[:, 0:1], scale=1.0 / D),
             reads=[mss.b, self.epsb.b], writes=[mss.b])
        S.op("vector", lambda e: e.reciprocal(out=mss.ap[:, 2:4], in_=mss.ap[:, 2:4]), reads=[mss.b], writes=[mss.b])
        for s_ in range(2):
            S.op("vector", lambda e, s_=s_: e.scalar_tensor_tensor(out=memn.ap[:, s_, :], in0=memt.ap[:, s_, :], scalar=mss.ap[:, 2 + s_:3 + s_], in1=gmem.ap,
                                                                   op0=ALU.mult, op1=ALU.mult), reads=[memt.b, gmem.b], sreads=[mss.b], writes=[memn.b])
        ev = 0
        for c in range(8):
            p = self.psum()
            pbv = p.ap.bitcast(BF16)
            for s_ in range(2):
                S.op("tensor", lambda e, pbv=pbv, s_=s_, c=c: e.transpose(out=pbv[:, s_ * 128:(s_ + 1) * 128], in_=memn.ap[:, s_, c * 128:(c + 1) * 128], identity=self.identb.ap),
                     reads=[memn.b, self.identb.b], writes=[p.b])
            self.evac(ev, memnT.ap[:, c, :], pbv[:, 0:256], [p.b], [memnT.b]); ev += 1
        for c in range(8):
            p = self.psum()
            self.proj_fm(p, wkv, c * 128, 128, memnT, N=256)
            self.evac(ev, mkT.ap[:, c, :], p.ap[:, 0:256], [p.b], [mkT.b]); ev += 1
        for s_ in range(2):
            for half in range(2):
                p = self.psum()
                self.proj_tm(p, memnT, s_ * 128, wkv, 1024 + half * 512, 512)
                self.evac(ev, mv.ap[:, s_, half * 512:(half + 1) * 512], p.ap, [p.b], [mv.b]); ev += 1
        S.barrier()
        ar.off = mark
        w_out = ar.alloc([128, 8, D], BF16, "w_out")
        wq = ar.alloc([128, 8, D], BF16, "wq")
        wo = ar.alloc([128, 8, D], BF16, "wo")
        self.load_w(w_out, w_out_dram)
        self.load_w(wq, self.xattn_wq[layer])
        self.load_w(wo, self.xattn_wo[layer])
        mix = [ar.alloc([128, 8, TS], BF16, f"mix{k}") for k in range(2)]
        xt = [ar.alloc([128, 8, TS], F32, f"xxt{k}") for k in range(2)]
        xn = ar.alloc([128, 8, TS], BF16, "xxn")
        sq = ar.alloc([128, 8, TS], BF16, "xsq")
        rs = ar.alloc([128, TS], F32, "xrs")
        qx = ar.alloc([128, 8, TS], BF16, "qx")
        att = ar.alloc([128, 8, TS], BF16, "att")
        pt = [ar.alloc([128, 2, TS], BF16, f"xpt{k}") for k in range(2)]
        rd = [ar.alloc([128, TS], F32, f"xrd{k}") for k in range(2)]
        mixv = self.MIX.ap.rearrange("(c p) t -> p c t", p=128)

        def stage1(i):
            m_ = mix[i % 2]
            x_ = xt[i % 2]
            S.dma("sync", lambda e: e.dma_start(out=m_.ap, in_=mixv[:, :, i * TS:(i + 1) * TS]), reads=[self.MIX.b], writes=[m_.b])
            S.dma("sync", lambda e: e.dma_start(out=x_.ap, in_=self.xt_tile_ap(i)), reads=[self.XT.b], writes=[x_.b])
            for jo in range(8):
                p = self.psum()
                self.proj_fm(p, w_out, jo * 128, 128, m_)
                S.op("vector", lambda e, p=p, jo=jo: e.tensor_tensor(out=x_.ap[:, jo, :], in0=p.ap, in1=x_.ap[:, jo, :], op=ALU.add),
                     reads=[p.b, x_.b], writes=[x_.b])

        stage1(0)
        for i in range(NT):
            x_ = xt[i % 2]
            if self.stop_after == f"mix{layer}":
                S.dma("sync", lambda e, i=i, x_=x_: e.dma_start(out=self.xt_tile_ap(i), in_=x_.ap), reads=[x_.b], writes=[self.XT.b], accum=True)
                if i + 1 < NT:
                    stage1(i + 1)
                continue
            self.rmsnorm(x_, xn, sq, rs, 2 + layer)
            for jo in range(8):
                p = self.psum()
                self.proj_fm(p, wq, jo * 128, 128, xn)
                self.evac(jo, qx.ap[:, jo, :], p.ap, [p.b], [qx.b], scale=1.0 / 16)
            if i + 1 < NT:
                stage1(i + 1)
            for h in range(4):
                pt_ = pt[h % 2]
                rd_ = rd[h % 2]
                for mb in range(2):
                    p = self.psum()
                    for dc in range(2):
                        S.op("tensor", lambda e, p=p, h=h, dc=dc, mb=mb: e.matmul(out=p.ap, lhsT=mkT.ap[:, 2 * h + dc, mb * 128:(mb + 1) * 128], rhs=qx.ap[:, 2 * h + dc, :],
                                                                                  start=(dc == 0), stop=(dc == 1)), reads=[mkT.b, qx.b], writes=[p.b])
                    S.op("scalar", lambda e, p=p, pt_=pt_, mb=mb: e.activation(out=pt_.ap[:, mb, :], in_=p.ap, func=AF.Exp), reads=[p.b], writes=[pt_.b])
                pd = self.psum()
                for mb in range(2):
                    S.op("tensor", lambda e, pd=pd, pt_=pt_, mb=mb: e.matmul(out=pd.ap, lhsT=self.onesb.ap, rhs=pt_.ap[:, mb, :], start=(mb == 0), stop=(mb == 1)),
                         reads=[self.onesb.b, pt_.b], writes=[pd.b])
                S.op("vector", lambda e, pd=pd, rd_=rd_: e.reciprocal(out=rd_.ap, in_=pd.ap), reads=[pd.b], writes=[rd_.b])
                for ec in range(2):
                    p = self.psum()
                    for mb in range(2):
                        S.op("tensor", lambda e, p=p, pt_=pt_, mb=mb, h=h, ec=ec: e.matmul(out=p.ap, lhsT=mv.ap[:, mb, h * 256 + ec * 128:h * 256 + (ec + 1) * 128], rhs=pt_.ap[:, mb, :],
                                                                                         start=(mb == 0), stop=(mb == 1)), reads=[mv.b, pt_.b], writes=[p.b])
                    S.op("vector", lambda e, p=p, rd_=rd_, h=h, ec=ec: e.tensor_tensor(out=att.ap[:, 2 * h + ec, :], in0=p.ap, in1=rd_.ap, op=ALU.mult),
                         reads=[p.b, rd_.b], writes=[att.b])
            for jo in range(8):
                p = self.psum()
                self.proj_fm(p, wo, jo * 128, 128, att)
                S.op("vector", lambda e, p=p, jo=jo, x_=x_: e.tensor_tensor(out=x_.ap[:, jo, :], in0=p.ap, in1=x_.ap[:, jo, :], op=ALU.add),
                     reads=[p.b, x_.b], writes=[x_.b])
            S.dma("sync", lambda e, i=i, x_=x_: e.dma_start(out=self.xt_tile_ap(i), in_=x_.ap), reads=[x_.b], writes=[self.XT.b], accum=True)
        S.barrier()

    def phase_ffn(self, layer, final=False):
        S = self.S
        ar = self.ar
        ar.reset()
        w_up = ar.alloc([128, 8, 2 * DFF], BF16, "w_up")
        w_dn = ar.alloc([128, NH, D], BF16, "w_dn")
        self.load_w(w_up, self.ffn_w_up[layer])
        self.load_w(w_dn, self.ffn_w_down[layer])
        cw = ar.alloc([128, 3, 44], F32, "cw")
        cb = ar.alloc([128, 44], F32, "cb")
        S.dma("sync", lambda e: e.dma_start(out=cw.ap, in_=self.ffn_conv_w[layer]), writes=[cw.b])
        S.dma("sync", lambda e: e.dma_start(out=cb.ap, in_=self.ffn_conv_b[layer]), writes=[cb.b])
        xt = ar.alloc([128, 8, TS], F32, "fxt")
        xn = ar.alloc([128, 8, TS], BF16, "fxn")
        rs = ar.alloc([128, TS], F32, "frs")
        araw = ar.raw[:, ar.off:ar.off + NH * TS // 2]
        ar.off += NH * TS // 2
        act = TT(araw.bitcast(BF16).rearrange("p (a b) -> p a b", a=NH), "fact")
        sq = TT(act.ap[:, 0:8, :], "fsq")
        sq.b = act.b
        hg = [ar.alloc([128, TS + 2], F32, f"hg{k}") for k in range(2)]
        hu = [ar.alloc([128, TS + 2], F32, f"hu{k}") for k in range(2)]
        yg = [ar.alloc([128, TS], F32, f"yg{k}") for k in range(2)]
        yu = [ar.alloc([128, TS], F32, f"yu{k}") for k in range(2)]
        halo = ar.alloc([128, 44, 2], F32, "halo")
        S.op("gpsimd", lambda e: e.memset(halo.ap, 0.0), writes=[halo.b])
        if final:
            otm = [TT(araw[:, 2048 + k * 1024:2048 + (k + 1) * 1024], f"otm{k}") for k in range(2)]
            for o_ in otm:
                o_.b = act.b
        u = 0
        for i in range(NT):
            S.dma("sync", lambda e, i=i: e.dma_start(out=xt.ap, in_=self.xt_tile_ap(i)), reads=[self.XT.b], writes=[xt.b])
            self.rmsnorm(xt, xn, sq, rs, 6 + layer)
            for j in range(NH):
                hbs = [hg[u % 2], hu[u % 2]]
                ybs = [yg[u % 2], yu[u % 2]]
                u += 1
                for br in range(2):
                    cidx = br * NH + j
                    hb = hbs[br]
                    yb = ybs[br]
                    p = self.psum()
                    self.proj_fm(p, w_up, br * DFF + j * 128, 128, xn)
                    S.op("scalar", lambda e, hb=hb, p=p: e.copy(out=hb.ap[:, 2:TS + 2], in_=p.ap), reads=[p.b], writes=[hb.b])
                    S.op("scalar", lambda e, yb=yb, p=p, cidx=cidx: e.activation(out=yb.ap, in_=p.ap, func=AF.Identity, bias=cb.ap[:, cidx:cidx + 1], scale=cw.ap[:, 2, cidx:cidx + 1]),
                         reads=[p.b, cw.b, cb.b], writes=[yb.b])
                    S.op("gpsimd", lambda e, hb=hb, cidx=cidx: e.tensor_copy(out=hb.ap[:, 0:2], in_=halo.ap[:, cidx, :]), reads=[halo.b], writes=[hb.b])
                    S.op("gpsimd", lambda e, hb=hb, cidx=cidx: e.tensor_copy(out=halo.ap[:, cidx, :], in_=hb.ap[:, TS:TS + 2]), reads=[hb.b], writes=[halo.b])
                for tap in (1, 0):
                    for br in range(2):
                        cidx = br * NH + j
                        hb = hbs[br]
                        yb = ybs[br]
                        S.op("vector", lambda e, hb=hb, yb=yb, cidx=cidx, tap=tap: e.scalar_tensor_tensor(out=yb.ap, in0=hb.ap[:, tap:tap + TS], scalar=cw.ap[:, tap, cidx:cidx + 1], in1=yb.ap,
                                                                                                 op0=ALU.mult, op1=ALU.add), reads=[hb.b, cw.b, yb.b], writes=[yb.b])
                g_ = ybs[0]
                u_ = ybs[1]
                S.op("scalar", lambda e, g_=g_: e.activation(out=g_.ap, in_=g_.ap, func=AF.Gelu_apprx_tanh), reads=[g_.b], writes=[g_.b])
                S.op("gpsimd", lambda e, g_=g_, u_=u_, j=j: e.tensor_tensor(out=act.ap[:, j, :], in0=g_.ap, in1=u_.ap, op=ALU.mult), reads=[g_.b, u_.b], writes=[act.b])
            for half in range(2):
                accs = [self.psum() for _ in range(4)]
                for j in range(NH):
                    for q4 in range(4):
                        jo = half * 4 + q4
                        p = accs[q4]
                        S.op("tensor", lambda e, p=p, j=j, jo=jo: e.matmul(out=p.ap, lhsT=w_dn.ap[:, j, jo * 128:(jo + 1) * 128], rhs=act.ap[:, j, :], start=(j == 0), stop=(j == NH - 1)),
                             reads=[w_dn.b, act.b], writes=[p.b])
                for q4 in range(4):
                    jo = half * 4 + q4
                    p = accs[q4]
                    S.op("vector", lambda e, p=p, jo=jo: e.tensor_tensor(out=xt.ap[:, jo, :], in0=p.ap, in1=xt.ap[:, jo, :], op=ALU.add), reads=[p.b, xt.b], writes=[xt.b])
            if not final:
                S.dma("sync", lambda e, i=i: e.dma_start(out=self.xt_tile_ap(i), in_=xt.ap), reads=[xt.b], writes=[self.XT.b], accum=True)
            else:
                self.final_out(i, xt, sq, rs, otm)
        S.barrier()

    def phase_ffn2(self, layer, final=False):
        S = self.S
        ar = self.ar
        ar.reset()
        TF = 256
        NTF = T // TF
        w_up = ar.alloc([128, 8, 2 * DFF], BF16, "w_up")
        w_dn = ar.alloc([128, NH, D], BF16, "w_dn")
        self.load_w(w_up, self.ffn_w_up[layer])
        self.load_w(w_dn, self.ffn_w_down[layer])
        cw = ar.alloc([128, 3, 44], F32, "cw")
        cb = ar.alloc([128, 44], F32, "cb")
        S.dma("sync", lambda e: e.dma_start(out=cw.ap, in_=self.ffn_conv_w[layer]), writes=[cw.b])
        S.dma("sync", lambda e: e.dma_start(out=cb.ap, in_=self.ffn_conv_b[layer]), writes=[cb.b])
        xt = [ar.alloc([128, 8, TF], F32, f"fxt{k}") for k in range(2)]
        xn = [ar.alloc([128, 8, TF], BF16, f"fxn{k}") for k in range(2)]
        sq = ar.alloc([128, 8, TF], BF16, "fsq")
        rs = [ar.alloc([128, TF], F32, f"frs{k}") for k in range(2)]
        NR = 4
        hb = [[ar.alloc([128, TF + 2], F32, f"hb{k}_{br}") for br in range(2)] for k in range(NR)]
        yb = [[ar.alloc([128, TF], F32, f"yb{k}_{br}") for br in range(2)] for k in range(NR)]
        actb = [ar.alloc([128, TF], BF16, f"actb{k}") for k in range(NR)]
        halo = ar.alloc([128, 44, 2], F32, "halo")
        S.op("gpsimd", lambda e: e.memset(halo.ap, 0.0), writes=[halo.b])
        if final:
            otm = [ar.alloc([128, D], F32, f"otm{k}") for k in range(2)]
            fsq = ar.alloc([128, 8, TF], BF16, "ffsq")
            frs = ar.alloc([128, TF], F32, "ffrs")
        accb = self.ps[0:4]
        ub = self.ps[4:7]
        nb = self.ps[7]
        xv = self.XT.ap.rearrange("(c p) t -> p c t", p=128)

        def load_norm(t):
            x_ = xt[t % 2]
            S.dma("sync", lambda e: e.dma_start(out=x_.ap, in_=xv[:, :, t * TF:(t + 1) * TF]), reads=[self.XT.b], writes=[x_.b])
            self.rmsnorm(x_, xn[t % 2], sq, rs[t % 2], 6 + layer, p=nb)

        def down(t, j):
            a_ = actb[j % NR]
            for jo in range(8):
                bank = accb[jo // 2]
                cs_ = (jo % 2) * TF
                st = (j == 0 and jo % 2 == 0)
                S.op("tensor", lambda e, bank=bank, cs_=cs_, jo=jo, st=st, a_=a_, j=j: e.matmul(out=bank.ap[:, cs_:cs_ + TF], lhsT=w_dn.ap[:, j, jo * 128:(jo + 1) * 128], rhs=a_.ap,
                                                                                       start=st, stop=(j == NH - 1), skip_group_check=True),
                     reads=[w_dn.b, a_.b], writes=[bank.b])

        load_norm(0)
        for t in range(NTF):
            x_ = xt[t % 2]
            xn_ = xn[t % 2]
            if t + 1 < NTF:
                load_norm(t + 1)
            for j in range(NH):
                U = ub[j % 3]
                k = j % NR
                for br in range(2):
                    cidx = br * NH + j
                    h_ = hb[k][br]
                    y_ = yb[k][br]
                    col = br * TF
                    for kc in range(8):
                        S.op("tensor", lambda e, kc=kc, U=U, col=col, br=br, j=j, xn_=xn_: e.matmul(out=U.ap[:, col:col + TF], lhsT=w_up.ap[:, kc, br * DFF + j * 128:br * DFF + (j + 1) * 128],
                                                                                  rhs=xn_.ap[:, kc, :], start=(kc == 0), stop=(kc == 7)),
                             reads=[w_up.b, xn_.b], writes=[U.b])
                    S.op("scalar", lambda e, h_=h_, U=U, col=col: e.copy(out=h_.ap[:, 2:TF + 2], in_=U.ap[:, col:col + TF]), reads=[U.b], writes=[h_.b])
                    S.op("gpsimd", lambda e, h_=h_, cidx=cidx: e.tensor_copy(out=h_.ap[:, 0:2], in_=halo.ap[:, cidx, :]), reads=[halo.b], writes=[h_.b])
                    S.op("gpsimd", lambda e, h_=h_, cidx=cidx: e.tensor_copy(out=halo.ap[:, cidx, :], in_=h_.ap[:, TF:TF + 2]), reads=[h_.b], writes=[halo.b])
                    S.op("gpsimd", lambda e, h_=h_, y_=y_, cidx=cidx: e.tensor_scalar(out=y_.ap, in0=h_.ap[:, 2:TF + 2], scalar1=cw.ap[:, 2, cidx:cidx + 1], scalar2=cb.ap[:, cidx:cidx + 1],
                                                                                     op0=ALU.mult, op1=ALU.add), reads=[h_.b, cw.b, cb.b], writes=[y_.b])
                for tap in (1, 0):
                    for br in range(2):
                        cidx = br * NH + j
                        h_ = hb[k][br]
                        y_ = yb[k][br]
                        S.op("vector", lambda e, h_=h_, y_=y_, cidx=cidx, tap=tap: e.scalar_tensor_tensor(out=y_.ap, in0=h_.ap[:, tap:tap + TF], scalar=cw.ap[:, tap, cidx:cidx + 1], in1=y_.ap,
                                                                                                 op0=ALU.mult, op1=ALU.add), reads=[h_.b, cw.b, y_.b], writes=[y_.b])
                g_ = yb[k][0]
                u_ = yb[k][1]
                a_ = actb[k]
                S.op("scalar", lambda e, g_=g_: e.activation(out=g_.ap, in_=g_.ap, func=AF.Gelu_apprx_tanh), reads=[g_.b], writes=[g_.b])
                S.op("gpsimd", lambda e, g_=g_, u_=u_, a_=a_: e.tensor_tensor(out=a_.ap, in0=g_.ap, in1=u_.ap, op=ALU.mult), reads=[g_.b, u_.b], writes=[a_.b])
                if j >= 2:
                    down(t, j - 2)
            down(t, NH - 2)
            down(t, NH - 1)
            for jo in range(8):
                bank = accb[jo // 2]
                cs_ = (jo % 2) * TF
                S.op("vector", lambda e, bank=bank, cs_=cs_, jo=jo, x_=x_: e.tensor_tensor(out=x_.ap[:, jo, :], in0=bank.ap[:, cs_:cs_ + TF], in1=x_.ap[:, jo, :], op=ALU.add),
                     reads=[bank.b, x_.b], writes=[x_.b])
            if not final:
                S.dma("sync", lambda e, t=t, x_=x_: e.dma_start(out=xv[:, :, t * TF:(t + 1) * TF], in_=x_.ap), reads=[x_.b], writes=[self.XT.b], accum=True)
            else:
                self.final_out(t, x_, fsq, frs, otm, p=nb, TN=TF)
        S.barrier()

    def final_out(self, i, xt, sq, rs, otm, p=None, TN=TS):
        S = self.S
        S.op("scalar", lambda e: e.activation(out=sq.ap, in_=xt.ap, func=AF.Square), reads=[xt.b], writes=[sq.b])
        p0 = p if p is not None else self.psum()
        for c in range(8):
            S.op("tensor", lambda e, c=c: e.matmul(out=p0.ap[:, 0:TN], lhsT=self.onesb.ap, rhs=sq.ap[:, c, :], start=(c == 0), stop=(c == 7)),
                 reads=[self.onesb.b, sq.b], writes=[p0.b])
        S.op("scalar", lambda e: e.activation(out=rs.ap, in_=p0.ap[:, 0:TN], func=AF.Sqrt, bias=self.epsb.ap[:, 0:1], scale=1.0 / D),
             reads=[p0.b, self.epsb.b], writes=[rs.b])
        S.op("vector", lambda e: e.reciprocal(out=rs.ap, in_=rs.ap), reads=[rs.b], writes=[rs.b])
        for c in range(8):
            S.op("vector", lambda e, c=c: e.scalar_tensor_tensor(out=xt.ap[:, c, :], in0=xt.ap[:, c, :], scalar=self.gain.ap[:, 8, c:c + 1], in1=rs.ap,
                                                                   op0=ALU.mult, op1=ALU.mult), reads=[xt.b, rs.b, self.gain.b], writes=[xt.b])
        ev = 0
        for s_ in range(TN // 128):
            o_ = otm[s_ % 2]
            for half in range(2):
                pp = p if p is not None else self.psum()
                for cc in range(4):
                    c = half * 4 + cc
                    S.op("tensor", lambda e, pp=pp, cc=cc, c=c, s_=s_: e.transpose(out=pp.ap[:, cc * 128:(cc + 1) * 128], in_=xt.ap[:, c, s_ * 128:(s_ + 1) * 128], identity=self.ident.ap),
                         reads=[xt.b, self.ident.b], writes=[pp.b])
                self.evac(ev, o_.ap[:, half * 512:(half + 1) * 512], pp.ap, [pp.b], [o_.b]); ev += 1
            S.dma("sync", lambda e, o_=o_, s_=s_: e.dma_start(out=self.out[i * TN + s_ * 128:i * TN + (s_ + 1) * 128, :], in_=o_.ap), reads=[o_.b], writes=[self.b_out], accum=True)

    def phase_l1_inproj(self):
        S = self.S
        ar = self.ar
        ar.reset()
        ipre = ar.alloc([4, T], F32, "ipre")
        lfn = ar.alloc([4, T], F32, "lfn1")
        self.m_ipre, self.m_lfn = ipre, lfn
        self.m_mark = ar.off
        w = ar.alloc([128, 8, 3080], BF16, "w_in1")
        self.load_w(w, self.mlstm_w_in)
        xt = [ar.alloc([128, 8, TS], F32, f"mxt{k}") for k in range(2)]
        xn = [ar.alloc([128, 8, TS], BF16, f"mxn{k}") for k in range(2)]
        sq = ar.alloc([128, 8, TS], BF16, "msq")
        rs = ar.alloc([128, TS], F32, "mrs")
        hb = [ar.alloc([128, TS + 3], F32, f"mhb{k}") for k in range(2)]
        yb = [ar.alloc([128, TS], F32, f"myb{k}") for k in range(2)]
        stg = [ar.alloc([128, TS], BF16, f"mstg{k}") for k in range(4)]
        vst = [ar.alloc([128, 4, 1028], BF16, f"mvst{k}") for k in range(2)]
        halo = ar.alloc([128, 8, 3], F32, "mhalo")
        mcw = ar.alloc([128, 4, 8], F32, "mcw")
        gb = ar.alloc([4, 2], F32, "gb")
        nbf = ar.alloc([4, 1], F32, "nbf1")
        tmp4 = ar.alloc([4, TS], F32, "tmp4")
        S.op("gpsimd", lambda e: e.memset(halo.ap, 0.0), writes=[halo.b])
        for k in range(2):
            S.op("gpsimd", lambda e, k=k: e.memset(vst[k].ap, 1.0), writes=[vst[k].b])
        S.dma("sync", lambda e: e.dma_start(out=mcw.ap, in_=self.mconv), writes=[mcw.b])
        S.dma("sync", lambda e: e.dma_start(out=gb.ap, in_=self.mgate_b), writes=[gb.b])
        S.op("vector", lambda e: e.tensor_scalar(out=nbf.ap, in0=gb.ap[:, 1:2], scalar1=-1.0, scalar2=None, op0=ALU.mult), reads=[gb.b], writes=[nbf.b])
        qscale = 128.0 ** -0.5
        u = 0
        si = 0
        ev = 0
        def prep1(i):
            x_ = xt[i % 2]
            S.dma("sync", lambda e: e.dma_start(out=x_.ap, in_=self.xt_tile_ap(i)), reads=[self.XT.b], writes=[x_.b])
            self.rmsnorm(x_, xn[i % 2], sq, rs, 1)

        prep1(0)
        for i in range(NT):
            x_ = xt[i % 2]
            xn_ = xn[i % 2]
            if i + 1 < NT:
                prep1(i + 1)
            for c in range(8):
                h_ = hb[u % 2]
                y_ = yb[u % 2]
                u += 1
                p = self.psum()
                self.proj_fm(p, w, c * 128, 128, xn_)
                S.op("scalar", lambda e, h_=h_, p=p: e.copy(out=h_.ap[:, 3:TS + 3], in_=p.ap), reads=[p.b], writes=[h_.b])
                S.op("gpsimd", lambda e, h_=h_, c=c: e.tensor_copy(out=h_.ap[:, 0:3], in_=halo.ap[:, c, :]), reads=[halo.b], writes=[h_.b])
                S.op("gpsimd", lambda e, h_=h_, c=c: e.tensor_copy(out=halo.ap[:, c, :], in_=h_.ap[:, TS:TS + 3]), reads=[h_.b], writes=[halo.b])
                S.op("vector", lambda e, h_=h_, y_=y_, c=c: e.tensor_scalar(out=y_.ap, in0=h_.ap[:, 3:TS + 3], scalar1=mcw.ap[:, 3, c:c + 1], scalar2=None, op0=ALU.mult),
                     reads=[h_.b, mcw.b], writes=[y_.b])
                for tap in range(3):
                    S.op("vector", lambda e, h_=h_, y_=y_, c=c, tap=tap: e.scalar_tensor_tensor(out=y_.ap, in0=h_.ap[:, tap:tap + TS], scalar=mcw.ap[:, tap, c:c + 1], in1=y_.ap,
                                                                                              op0=ALU.mult, op1=ALU.add), reads=[h_.b, mcw.b, y_.b], writes=[y_.b])
                st = stg[si % 4]; si += 1
                if c < 4:
                    S.op("scalar", lambda e, y_=y_: e.activation(out=y_.ap, in_=y_.ap, func=AF.Silu), reads=[y_.b], writes=[y_.b])
                    S.op("vector", lambda e, y_=y_, st=st: e.tensor_scalar(out=st.ap, in0=y_.ap, scalar1=qscale, scalar2=None, op0=ALU.mult), reads=[y_.b], writes=[st.b])
                else:
                    S.op("scalar", lambda e, y_=y_, st=st: e.activation(out=st.ap, in_=y_.ap, func=AF.Silu), reads=[y_.b], writes=[st.b])
                S.dma("sync", lambda e, st=st, c=c, i=i: e.dma_start(out=self.QK.ap[c, :, i * TS:(i + 1) * TS], in_=st.ap), reads=[st.b], writes=[self.QK.b], accum=True)
            vs = vst[i % 2]
            for s_ in range(4):
                for half in range(2):
                    p = self.psum()
                    self.proj_tm(p, xn_, s_ * 128, w, 1024 + half * 512, 512)
                    ov = vs.ap[:, s_, :].rearrange("p (h c) -> p h c", c=257)[:, 2 * half:2 * half + 2, 0:256]
                    pv = p.ap.rearrange("p (h c) -> p h c", c=256)
                    self.evac(ev, ov, pv, [p.b], [vs.b]); ev += 1
            S.dma("sync", lambda e, vs=vs, i=i: e.dma_start(out=self.VM.ap[i * TS:(i + 1) * TS, :].rearrange("(s p) c -> p s c", p=128), in_=vs.ap),
                  reads=[vs.b], writes=[self.VM.b], accum=True)
            p = self.psum()
            self.proj_fm(p, w, 2048, 4, xn_)
            S.op("scalar", lambda e, p=p, i=i: e.activation(out=ipre.ap[:, i * TS:(i + 1) * TS], in_=p.ap[0:4, :], func=AF.Identity, bias=gb.ap[:, 0:1], scale=1.0),
                 reads=[p.b, gb.b], writes=[ipre.b])
            p = self.psum()
            self.proj_fm(p, w, 2052, 4, xn_)
            S.op("scalar", lambda e, p=p: e.activation(out=tmp4.ap, in_=p.ap[0:4, :], func=AF.Exp, bias=nbf.ap[:, 0:1], scale=-1.0), reads=[p.b, nbf.b], writes=[tmp4.b])
            S.op("scalar", lambda e, i=i: e.activation(out=lfn.ap[:, i * TS:(i + 1) * TS], in_=tmp4.ap, func=AF.Ln, bias=1.0, scale=1.0), reads=[tmp4.b], writes=[lfn.b])
            for c in range(8):
                p = self.psum()
                self.proj_fm(p, w, 2056 + c * 128, 128, xn_)
                st = stg[si % 4]; si += 1
                S.op("scalar", lambda e, p=p, st=st: e.activation(out=st.ap, in_=p.ap, func=AF.Sigmoid), reads=[p.b], writes=[st.b])
                S.dma("sync", lambda e, st=st, c=c, i=i: e.dma_start(out=self.OG.ap[c * 128:(c + 1) * 128, i * TS:(i + 1) * TS], in_=st.ap), reads=[st.b], writes=[self.OG.b], accum=True)
        S.barrier()

    def phase_mlstm(self):
        S = self.S
        ar = self.ar
        ar.off = self.m_mark
        ipre, lfn = self.m_ipre, self.m_lfn
        V = "vector"
        negM = ar.alloc([4, T], F32, "mnegM")
        emarg = ar.alloc([4, T], F32, "memarg")
        a_tm = ar.alloc([128, NB, 4], F32, "a_tm")
        MEND = ar.alloc([128, 4, NB + 1], F32, "MEND")
        hn = ar.alloc([128, 8], F32, "hn")
        mark2 = ar.off
        cs = ar.alloc([4, T], F32, "mcs")
        a = ar.alloc([4, T], F32, "ma")
        Mx = ar.alloc([4, T], F32, "mMx")
        ones4 = ar.alloc([4, T], F32, "mones4")
        S.dma("sync", lambda e: e.dma_start(out=hn.ap, in_=self.mhn), writes=[hn.b])
        S.op("gpsimd", lambda e: e.memset(ones4.ap, 1.0), writes=[ones4.b])
        S.op("gpsimd", lambda e: e.memset(MEND.ap, 0.0), writes=[MEND.b])
        for i in range(NT):
            sl = slice(i * TS, (i + 1) * TS)
            init = 0.0 if i == 0 else cs.ap[:, i * TS - 1:i * TS]
            S.op(V, lambda e, sl=sl, init=init: e.tensor_tensor_scan(out=cs.ap[:, sl], data0=ones4.ap[:, sl], data1=lfn.ap[:, sl], initial=init, op0=ALU.mult, op1=ALU.add),
                 reads=[ones4.b, lfn.b], sreads=[cs.b], writes=[cs.b])
        S.op(V, lambda e: e.tensor_tensor(out=a.ap, in0=ipre.ap, in1=cs.ap, op=ALU.add), reads=[ipre.b, cs.b], writes=[a.b])
        for i in range(NT):
            sl = slice(i * TS, (i + 1) * TS)
            init = 0.0 if i == 0 else Mx.ap[:, i * TS - 1:i * TS]
            S.op(V, lambda e, sl=sl, init=init: e.tensor_tensor_scan(out=Mx.ap[:, sl], data0=ones4.ap[:, sl], data1=a.ap[:, sl], initial=init, op0=ALU.mult, op1=ALU.max),
                 reads=[ones4.b, a.b], sreads=[Mx.b], writes=[Mx.b])
        S.op(V, lambda e: e.tensor_scalar(out=negM.ap, in0=Mx.ap, scalar1=-1.0, scalar2=None, op0=ALU.mult), reads=[Mx.b], writes=[negM.b])
        S.op(V, lambda e: e.tensor_tensor(out=emarg.ap, in0=cs.ap, in1=Mx.ap, op=ALU.subtract), reads=[cs.b, Mx.b], writes=[emarg.b])
        for g in range(4):
            p = self.psum()
            for bb in range(8):
                b_ = g * 8 + bb
                S.op("tensor", lambda e, p=p, bb=bb, b_=b_: e.transpose(out=p.ap[:, bb * 4:(bb + 1) * 4], in_=a.ap[0:4, b_ * 128:(b_ + 1) * 128], identity=self.ident.ap[0:4, 0:4]),
                     reads=[a.b, self.ident.b], writes=[p.b])
            S.op(V, lambda e, p=p, g=g: e.tensor_copy(out=a_tm.ap[:, g * 8:(g + 1) * 8, :], in_=p.ap[:, 0:32].rearrange("p (b h) -> p b h", h=4)), reads=[p.b], writes=[a_tm.b])
        S.barrier()
        ar.off = mark2
        C32 = [ar.alloc([128, 257], F32, f"C32_{h}") for h in range(4)]
        Cbf = [ar.alloc([128, 256], BF16, f"Cbf_{h}") for h in range(4)]
        nrep = [ar.alloc([128, 128], BF16, f"nrep_{h}") for h in range(4)]
        for h in range(4):
            S.op("gpsimd", lambda e, h=h: e.memset(C32[h].ap, 0.0), writes=[C32[h].b])
            S.op("gpsimd", lambda e, h=h: e.memset(Cbf[h].ap, 0.0), writes=[Cbf[h].b])
            S.op("gpsimd", lambda e, h=h: e.memset(nrep[h].ap, 0.0), writes=[nrep[h].b])
        NMt = [ar.alloc([128, TS], F32, f"NMt{h}") for h in range(4)]
        EMt = [ar.alloc([128, TS], F32, f"EMt{h}") for h in range(4)]
        qT = [ar.alloc([128, TS], BF16, f"mqT{h}") for h in range(4)]
        kT = [ar.alloc([128, TS], BF16, f"mkT{h}") for h in range(4)]
        ogt = [ar.alloc([128, 2, TS], BF16, f"ogt{h}") for h in range(4)]
        hT = [ar.alloc([128, 2, TS], F32, f"hT{h}") for h in range(4)]
        vt = [ar.alloc([128, 4, 1028], BF16, f"mvt{k}") for k in range(2)]
        Wt = [ar.alloc([128, 128], F32, f"Wt{k}") for k in range(4)]
        Wm = [ar.alloc([128, 128], F32, f"Wm{k}") for k in range(4)]
        At = [ar.alloc([128, 128], BF16, f"At{k}") for k in range(4)]
        wint = [ar.alloc([128, 128], F32, f"wint{k}") for k in range(4)]
        qt = [ar.alloc([128, 128], BF16, f"qt{k}") for k in range(4)]
        kt = [ar.alloc([128, 128], BF16, f"kt{k}") for k in range(4)]
        wk = [ar.alloc([128, 2], F32, f"wk{k}") for k in range(4)]
        dd = [ar.alloc([128, 128], F32, f"dd{k}") for k in range(4)]
        sq2 = ar.alloc([128, 2, TS], BF16, "sq2")
        rs = ar.alloc([128, TS], F32, "mrs2")
        tmpf = ar.alloc([128, 2, TS], F32, "tmpf")
        ostg = [ar.alloc([128, 2, TS], BF16, f"mostg{k}") for k in range(2)]
        u = 0
        oc = 0
        for i in range(NT):
            sl = slice(i * TS, (i + 1) * TS)
            v_ = vt[i % 2]
            S.dma("sync", lambda e, v_=v_, i=i: e.dma_start(out=v_.ap, in_=self.VM.ap[i * TS:(i + 1) * TS, :].rearrange("(s p) c -> p s c", p=128)),
                  reads=[self.VM.b], writes=[v_.b])
            for h in range(4):
                S.dma("sync", lambda e, h=h, sl=sl: e.dma_start(out=qT[h].ap, in_=self.QK.ap[h, :, sl]), reads=[self.QK.b], writes=[qT[h].b])
                S.dma("sync", lambda e, h=h, sl=sl: e.dma_start(out=kT[h].ap, in_=self.QK.ap[4 + h, :, sl]), reads=[self.QK.b], writes=[kT[h].b])
                S.dma("sync", lambda e, h=h, sl=sl: e.dma_start(out=ogt[h].ap, in_=self.OG.ap[h * 256:(h + 1) * 256, sl].rearrange("(e p) t -> p e t", p=128)),
                      reads=[self.OG.b], writes=[ogt[h].b])
                p = self.psum()
                S.op("tensor", lambda e, p=p, h=h, sl=sl: e.matmul(out=p.ap, lhsT=self.sel.ap[0:4, h, :], rhs=negM.ap[0:4, sl], start=True, stop=True),
                     reads=[self.sel.b, negM.b], writes=[p.b])
                S.op("scalar", lambda e, p=p, h=h: e.copy(out=NMt[h].ap, in_=p.ap), reads=[p.b], writes=[NMt[h].b])
                S.op(V, lambda e, h=h, i=i: e.tensor_scalar(out=MEND.ap[:, h, 4 * i + 1:4 * i + 5], in0=NMt[h].ap.rearrange("p (c k) -> p c k", k=128)[:, :, 127],
                                                           scalar1=-1.0, scalar2=None, op0=ALU.mult), reads=[NMt[h].b], writes=[MEND.b])
                p = self.psum()
                S.op("tensor", lambda e, p=p, h=h, sl=sl: e.matmul(out=p.ap, lhsT=self.sel.ap[0:4, h, :], rhs=emarg.ap[0:4, sl], start=True, stop=True),
                     reads=[self.sel.b, emarg.b], writes=[p.b])
                S.op("scalar", lambda e, p=p, h=h: e.activation(out=EMt[h].ap, in_=p.ap, func=AF.Exp), reads=[p.b], writes=[EMt[h].b])
            for cc in range(4):
                c = 4 * i + cc
                cols = slice(cc * 128, (cc + 1) * 128)
                b1 = [None] * 4
                b2 = [None] * 4
                for h in range(4):
                    b1[h] = self.psum()
                    ptv = b1[h].ap.bitcast(BF16)
                    S.op("tensor", lambda e, bb=b1[h], h=h, cols=cols: e.matmul(out=bb.ap[:, 384:512], lhsT=kT[h].ap[:, cols], rhs=qT[h].ap[:, cols], start=True, stop=True),
                         reads=[kT[h].b, qT[h].b], writes=[b1[h].b])
                    S.op("tensor", lambda e, ptv=ptv, h=h, cols=cols: e.transpose(out=ptv[:, 520:648], in_=kT[h].ap[:, cols], identity=self.identb.ap),
                         reads=[kT[h].b, self.identb.b], writes=[b1[h].b])
                for h in range(4):
                    Mo = MEND.ap[:, h, c:c + 1]
                    negMn = NMt[h].ap[:, cc * 128 + 127:cc * 128 + 128]
                    acol = a_tm.ap[:, c, h:h + 1]
                    ptv = b1[h].ap.bitcast(BF16)
                    S.op("scalar", lambda e, h=h, cols=cols, acol=acol: e.activation(out=Wt[h].ap, in_=NMt[h].ap[:, cols], func=AF.Exp, bias=acol, scale=1.0),
                         reads=[NMt[h].b], sreads=[a_tm.b], writes=[Wt[h].b])
                    S.op("scalar", lambda e, h=h, cols=cols, Mo=Mo: e.activation(out=wint[h].ap, in_=NMt[h].ap[:, cols], func=AF.Exp, bias=Mo, scale=1.0),
                         reads=[NMt[h].b], sreads=[MEND.b], writes=[wint[h].b])
                    S.op("scalar", lambda e, h=h, acol=acol, negMn=negMn: e.activation(out=wk[h].ap[:, 0:1], in_=acol, func=AF.Exp, bias=negMn, scale=1.0),
                         reads=[a_tm.b], sreads=[NMt[h].b], writes=[wk[h].b])
                    S.op("scalar", lambda e, h=h, Mo=Mo, negMn=negMn: e.activation(out=wk[h].ap[:, 1:2], in_=Mo, func=AF.Exp, bias=negMn, scale=1.0),
                         reads=[MEND.b], sreads=[NMt[h].b], writes=[wk[h].b])
                    S.op("gpsimd", lambda e, h=h: e.tensor_tensor(out=Wm[h].ap, in0=Wt[h].ap, in1=self.tri01.ap, op=ALU.mult), reads=[Wt[h].b, self.tri01.b], writes=[Wm[h].b])
                    S.op("gpsimd", lambda e, h=h, cols=cols: e.tensor_tensor(out=qt[h].ap, in0=qT[h].ap[:, cols], in1=wint[h].ap, op=ALU.mult),
                         reads=[qT[h].b, wint[h].b], writes=[qt[h].b])
                    S.op(V, lambda e, h=h, bb=b1[h]: e.tensor_tensor(out=At[h].ap, in0=bb.ap[:, 384:512], in1=Wm[h].ap, op=ALU.mult), reads=[b1[h].b, Wm[h].b], writes=[At[h].b])
                    S.op(V, lambda e, h=h, ptv=ptv: e.tensor_scalar(out=kt[h].ap, in0=ptv[:, 520:648], scalar1=wk[h].ap[:, 0:1], scalar2=None, op0=ALU.mult),
                         reads=[b1[h].b], sreads=[wk[h].b], writes=[kt[h].b])
                for h in range(4):
                    b2[h] = self.psum()
                    ps_n = b2[h]
                    for e_ in range(2):
                        S.op("tensor", lambda e, ps_n=ps_n, e_=e_, h=h: e.matmul(out=ps_n.ap[:, e_ * 128:(e_ + 1) * 128], lhsT=Cbf[h].ap[:, e_ * 128:(e_ + 1) * 128], rhs=qt[h].ap,
                                                                          start=True, stop=False), reads=[Cbf[h].b, qt[h].b], writes=[ps_n.b])
                        S.op("tensor", lambda e, ps_n=ps_n, e_=e_, h=h, cc=cc, v_=v_: e.matmul(out=ps_n.ap[:, e_ * 128:(e_ + 1) * 128],
                                                                                        lhsT=v_.ap[:, cc, h * 257 + e_ * 128:h * 257 + (e_ + 1) * 128], rhs=At[h].ap,
                                                                                        start=False, stop=True), reads=[v_.b, At[h].b], writes=[ps_n.b])
                    S.op("tensor", lambda e, ps_n=ps_n, h=h: e.matmul(out=ps_n.ap[:, 256:384], lhsT=nrep[h].ap, rhs=qt[h].ap, start=True, stop=False),
                         reads=[nrep[h].b, qt[h].b], writes=[ps_n.b])
                    S.op("tensor", lambda e, ps_n=ps_n, h=h: e.matmul(out=ps_n.ap[:, 256:384], lhsT=self.onesb.ap, rhs=At[h].ap, start=False, stop=True),
                         reads=[self.onesb.b, At[h].b], writes=[ps_n.b])
                    S.op("tensor", lambda e, bb=b1[h], h=h, cc=cc, v_=v_: e.matmul(out=bb.ap[:, 0:257], lhsT=kt[h].ap, rhs=v_.ap[:, cc, h * 257:(h + 1) * 257], start=True, stop=True),
                         reads=[kt[h].b, v_.b], writes=[b1[h].b])
                for h in range(4):
                    ps_n = b2[h]
                    S.op("scalar", lambda e, ps_n=ps_n, h=h: e.activation(out=dd[h].ap, in_=ps_n.ap[:, 256:384], func=AF.Abs), reads=[ps_n.b], writes=[dd[h].b])
                    S.op(V, lambda e, h=h, cols=cols: e.tensor_tensor(out=dd[h].ap, in0=dd[h].ap, in1=EMt[h].ap[:, cols], op=ALU.max),
                         reads=[dd[h].b, EMt[h].b], writes=[dd[h].b])
                    S.op(V, lambda e, h=h: e.reciprocal(out=dd[h].ap, in_=dd[h].ap), reads=[dd[h].b], writes=[dd[h].b])
                    for e_ in range(2):
                        S.op(V, lambda e, ps_n=ps_n, h=h, cols=cols, e_=e_: e.tensor_tensor(out=hT[h].ap[:, e_, cols], in0=ps_n.ap[:, e_ * 128:(e_ + 1) * 128], in1=dd[h].ap, op=ALU.mult),
                             reads=[ps_n.b, dd[h].b], writes=[hT[h].b])
                    S.op(V, lambda e, bb=b1[h], h=h: e.scalar_tensor_tensor(out=C32[h].ap, in0=C32[h].ap, scalar=wk[h].ap[:, 1:2], in1=bb.ap[:, 0:257], op0=ALU.mult, op1=ALU.add),
                         reads=[C32[h].b, b1[h].b], sreads=[wk[h].b], writes=[C32[h].b])
                    S.op("scalar", lambda e, h=h: e.copy(out=Cbf[h].ap, in_=C32[h].ap[:, 0:256]), reads=[C32[h].b], writes=[Cbf[h].b])
                    S.op("gpsimd", lambda e, h=h: e.tensor_copy(out=nrep[h].ap, in_=C32[h].ap[:, 256:257].to_broadcast([128, 128])), reads=[C32[h].b], writes=[nrep[h].b])
            for h in range(4):
                S.op("scalar", lambda e, h=h: e.activation(out=sq2.ap, in_=hT[h].ap, func=AF.Square), reads=[hT[h].b], writes=[sq2.b])
                pm = self.psum()
                for e_ in range(2):
                    S.op("tensor", lambda e, pm=pm, e_=e_: e.matmul(out=pm.ap, lhsT=self.onesb.ap, rhs=sq2.ap[:, e_, :], start=(e_ == 0), stop=(e_ == 1)),
                         reads=[self.onesb.b, sq2.b], writes=[pm.b])
                S.op("scalar", lambda e, pm=pm: e.activation(out=rs.ap, in_=pm.ap, func=AF.Sqrt, bias=self.epsb.ap[:, 0:1], scale=1.0 / 256), reads=[pm.b, self.epsb.b], writes=[rs.b])
                S.op(V, lambda e: e.reciprocal(out=rs.ap, in_=rs.ap), reads=[rs.b], writes=[rs.b])
                os_ = ostg[oc % 2]
                oc += 1
                for e_ in range(2):
                    S.op(V, lambda e, h=h, e_=e_: e.scalar_tensor_tensor(out=tmpf.ap[:, e_, :], in0=hT[h].ap[:, e_, :], scalar=hn.ap[:, 2 * h + e_:2 * h + e_ + 1], in1=rs.ap,
                                                                       op0=ALU.mult, op1=ALU.mult), reads=[hT[h].b, rs.b, hn.b], writes=[tmpf.b])
                S.op("gpsimd", lambda e, h=h, os_=os_: e.tensor_tensor(out=os_.ap, in0=tmpf.ap, in1=ogt[h].ap, op=ALU.mult), reads=[tmpf.b, ogt[h].b], writes=[os_.b])
                S.dma("sync", lambda e, h=h, os_=os_, sl=sl: e.dma_start(out=self.MIX.ap[h * 256:(h + 1) * 256, sl].rearrange("(e p) t -> p e t", p=128), in_=os_.ap),
                      reads=[os_.b], writes=[self.MIX.b], accum=True)
        S.barrier()

    def phase_dump(self):
        S = self.S
        ar = self.ar
        ar.reset()
        xt = ar.alloc([128, 8, TS], F32, "dxt")
        otm = [ar.alloc([128, D], F32, f"dotm{k}") for k in range(2)]
        for i in range(NT):
            S.dma("sync", lambda e, i=i: e.dma_start(out=xt.ap, in_=self.xt_tile_ap(i)), reads=[self.XT.b], writes=[xt.b])
            ev = 0
            for s_ in range(4):
                o_ = otm[s_ % 2]
                for half in range(2):
                    p = self.psum()
                    for cc in range(4):
                        c = half * 4 + cc
                        S.op("tensor", lambda e, p=p, cc=cc, c=c, s_=s_: e.transpose(out=p.ap[:, cc * 128:(cc + 1) * 128], in_=xt.ap[:, c, s_ * 128:(s_ + 1) * 128], identity=self.ident.ap),
                             reads=[xt.b, self.ident.b], writes=[p.b])
                    self.evac(ev, o_.ap[:, half * 512:(half + 1) * 512], p.ap, [p.b], [o_.b]); ev += 1
                S.dma("sync", lambda e, o_=o_, i=i, s_=s_: e.dma_start(out=self.out[i * TS + s_ * 128:i * TS + (s_ + 1) * 128, :], in_=o_.ap), reads=[o_.b], writes=[self.b_out], accum=True)
        S.barrier()

    def build(self):
        sa = self.stop_after
        self.phase_l0_inproj()
        if sa == "inproj0":
            self.S.emit(); return self.nc
        self.phase_fox()
        self.phase_diff()
        self.phase_outproj_xattn(0, self.attn_w_out)
        if sa in ("mix0", "xat0"):
            self.phase_dump(); self.S.emit(); return self.nc
        self.phase_ffn(0)
        if sa == "ffn0":
            self.phase_dump(); self.S.emit(); return self.nc
        self.phase_l1_inproj()
        self.phase_mlstm()
        self.phase_outproj_xattn(1, self.mlstm_w_out)
        if sa in ("mix1", "xat1"):
            self.phase_dump(); self.S.emit(); return self.nc
        self.phase_ffn(1, final=True)
        self.S.emit()
        return self.nc


def make_consts():
    k = np.arange(128)[:, None]
    q = np.arange(128)[None, :]
    c = {}
    c["c_ident"] = np.eye(128, dtype=np.float32)
    c["c_trineg"] = np.where(q >= k, 0.0, NEG).astype(np.float32)
    c["c_dneg"] = np.where((k >= 64) & (q < 64), NEG, 0.0).astype(np.float32)
    c["c_tri01"] = (q >= k).astype(np.float32)
    sel = np.zeros((4, 4, 128), np.float32)
    for h in range(4):
        sel[h, h, :] = 1.0
    c["c_sel"] = sel
    return c


def make_in_maps(inputs, ncores=8):
    f = lambda a: np.ascontiguousarray(np.asarray(a, dtype=np.float32))
    shared = dict(
        gains=f(np.concatenate([inputs["mix_norm"], inputs["xattn_norm"], inputs["mem_norm"], inputs["ffn_norm"], np.asarray(inputs["final_norm"])[None, :]],
                               axis=0).reshape(9, 8, 128).transpose(2, 0, 1)),
        memgain=f(inputs["mem_norm"]),
        attn_w_in=f(inputs["attn_w_in"][0]), attn_w_out=f(inputs["attn_w_out"][0]),
        mlstm_w_in=f(inputs["mlstm_w_in"][0]), mlstm_w_out=f(inputs["mlstm_w_out"][0]),
        xattn_wq=f(inputs["xattn_wq"]), xattn_wkv=f(inputs["xattn_wkv"]), xattn_wo=f(inputs["xattn_wo"]),
        ffn_w_up=f(inputs["ffn_w_up"]), ffn_w_down=f(inputs["ffn_w_down"]),
        ffn_conv_w=f(np.asarray(inputs["ffn_conv_w"]).reshape(2, 3, 44, 128).transpose(0, 3, 1, 2)),
        ffn_conv_b=f(np.asarray(inputs["ffn_conv_b"]).reshape(2, 44, 128).transpose(0, 2, 1)),
        fox_bf=f(np.asarray(inputs["attn_fox_bf"]).reshape(8, 1)),
        lam_in=f(np.stack([inputs["diff_lq1"][0], inputs["diff_lk1"][0], inputs["diff_lq2"][0], inputs["diff_lk2"][0]], axis=0)),
        subln=f(np.asarray(inputs["diff_subln"]).reshape(128, 1)),
        mconv=f(np.asarray(inputs["mlstm_conv_qk"][0]).reshape(4, 8, 128).transpose(2, 0, 1)),
        mgate_b=f(np.stack([inputs["mlstm_b_i"][0], inputs["mlstm_b_f"][0]], axis=1)),
        mhn=f(np.asarray(inputs["mlstm_head_norm"][0]).reshape(8, 128).T),
    )
    shared.update(make_consts())
    maps = []
    for c in range(ncores):
        b = c % 4
        m = dict(shared)
        m["x"] = f(inputs["x"][b])
        m["mem"] = f(inputs["mem"][b])
        maps.append(m)
    return maps


_CACHE = {}


def kernel(**inputs):
    if "nc" not in _CACHE:
        _CACHE["nc"] = Builder().build()
    nc = _CACHE["nc"]
    maps = make_in_maps(inputs, 8)
    res = run_bass_kernel_spmd(nc, maps, core_ids=list(range(8)))
    out = np.stack([np.asarray(res.results[b]["out"]) for b in range(4)], axis=0)
    return out.astype(np.float32)
```
